# Optimizing a Trainium2 kernel written in Bass

```python
import jax, jax.numpy as jnp
from jax import lax
import numpy as np

D_MODEL = 1024
BATCH = 4
SEQ = 8192
DEPTH = 1

D_RNN = D_MODEL
RNN_HEADS = 8
RNN_HEAD_DIM = D_RNN // RNN_HEADS
RNN_CONV_WIDTH = 4
RGLRU_C = 8.0
D_CONV = D_MODEL
CONF_KERNEL = 31
D_FF = ((8 * D_MODEL // 3 + 127) // 128) * 128
N_BRANCH = 2
N_SUBLAYER = 3
FFN_RES_WEIGHT = 0.5
DEEPNORM_ALPHA = (2.0 * DEPTH) ** 0.25
DEEPNORM_BETA = (8.0 * DEPTH) ** -0.25
LN_EPS = 1e-5
IN_SPLITS = (D_RNN, 2 * D_RNN, 2 * D_RNN + 2 * D_CONV, 2 * D_RNN + 2 * D_CONV + D_MODEL)
D_IN = 2 * D_RNN + 2 * D_CONV + N_BRANCH * D_MODEL

kernel_name = "hybrid_rglru_conformer_deepnorm_adaln"


def layer_norm(x, g, b):
    xf = x.astype(jnp.float32)
    mu = jnp.mean(xf, axis=-1, keepdims=True)
    xc = xf - mu
    var = jnp.mean(jnp.square(xc), axis=-1, keepdims=True)
    y = (xc * lax.rsqrt(var + LN_EPS)).astype(x.dtype)
    return y * g + b


def modulate(x, shift, scale):
    return x * (1 + scale[:, None, :]) + shift[:, None, :]


def causal_depthwise_conv(x, w, b):
    k = w.shape[0]
    y = lax.conv_general_dilated(
        x, w[:, None, :], window_strides=(1,), padding=[(k - 1, 0)],
        dimension_numbers=('NWC', 'WIO', 'NWC'), feature_group_count=x.shape[-1])
    return y + b


def swiglu(u, w_gate, w_up, w_down):
    return (jax.nn.silu(u @ w_gate) * (u @ w_up)) @ w_down


def rg_lru(x, w_r, b_r, w_i, b_i, lam):
    bsz, seq, _ = x.shape
    xh = x.reshape(bsz, seq, RNN_HEADS, RNN_HEAD_DIM)
    r = jax.nn.sigmoid(jnp.einsum('bshd,hde->bshe', xh, w_r).reshape(bsz, seq, D_RNN) + b_r)
    i = jax.nn.sigmoid(jnp.einsum('bshd,hde->bshe', xh, w_i).reshape(bsz, seq, D_RNN) + b_i)
    log_a = (-RGLRU_C * jax.nn.softplus(-lam.astype(jnp.float32))) * r.astype(jnp.float32)
    a = jnp.exp(log_a)
    bterm = jnp.sqrt(-jnp.expm1(2.0 * log_a)) * (i * x).astype(jnp.float32)

    def combine(left, right):
        a1, b1 = left
        a2, b2 = right
        return a1 * a2, a2 * b1 + b2

    _, h = lax.associative_scan(combine, (a, bterm), axis=1)
    return h.astype(x.dtype)


def setup_inputs(seed: int = 0) -> dict:
    key = jax.random.key(seed)
    ks = jax.random.split(key, 40)
    f32 = jnp.float32

    def nrm(k, shape, scale):
        return jax.random.normal(k, shape, f32) * scale

    def gain(k, n):
        return 1.0 + 0.02 * jax.random.normal(k, (n,), f32)

    a0 = jax.random.uniform(ks[14], (D_RNN,), f32, 0.9, 0.999) ** (1.0 / RGLRU_C)
    lam = jnp.log(a0) - jnp.log1p(-a0)
    return {
        'x': nrm(ks[0], (BATCH, SEQ, D_MODEL), 1.0),
        'c': nrm(ks[1], (BATCH, D_MODEL), 1.0),
        'ada_w': nrm(ks[2], (D_MODEL, 3 * N_SUBLAYER * D_MODEL), 0.2 * D_MODEL ** -0.5),
        'ada_b': nrm(ks[3], (3 * N_SUBLAYER * D_MODEL,), 0.01),
        'ffn1_w_gate': nrm(ks[4], (D_MODEL, D_FF), D_MODEL ** -0.5),
        'ffn1_w_up': nrm(ks[5], (D_MODEL, D_FF), D_MODEL ** -0.5),
        'ffn1_w_down': nrm(ks[6], (D_FF, D_MODEL), DEEPNORM_BETA * D_FF ** -0.5),
        'ln1_g': gain(ks[7], D_MODEL),
        'ln1_b': nrm(ks[8], (D_MODEL,), 0.02),
        'w_in': nrm(ks[9], (D_MODEL, D_IN), D_MODEL ** -0.5),
        'rnn_conv_w': nrm(ks[10], (RNN_CONV_WIDTH, D_RNN), RNN_CONV_WIDTH ** -0.5),
        'rnn_conv_b': nrm(ks[11], (D_RNN,), 0.01),
        'rglru_w_r': nrm(ks[12], (RNN_HEADS, RNN_HEAD_DIM, RNN_HEAD_DIM), RNN_HEAD_DIM ** -0.5),
        'rglru_b_r': nrm(ks[13], (D_RNN,), 0.01),
        'rglru_w_i': nrm(ks[15], (RNN_HEADS, RNN_HEAD_DIM, RNN_HEAD_DIM), RNN_HEAD_DIM ** -0.5),
        'rglru_b_i': nrm(ks[16], (D_RNN,), 0.01),
        'rglru_lambda': lam,
        'rnn_w_proj': nrm(ks[17], (D_RNN, D_MODEL), D_RNN ** -0.5),
        'conf_dw_w': nrm(ks[18], (CONF_KERNEL, D_CONV), CONF_KERNEL ** -0.5),
        'conf_dw_b': nrm(ks[19], (D_CONV,), 0.01),
        'conf_ln_g': gain(ks[20], D_CONV),
        'conf_ln_b': nrm(ks[21], (D_CONV,), 0.02),
        'conf_w_proj': nrm(ks[22], (D_CONV, D_MODEL), D_CONV ** -0.5),
        'mix_w_out': nrm(ks[23], (D_MODEL, D_MODEL), DEEPNORM_BETA * D_MODEL ** -0.5),
        'ln2_g': gain(ks[24], D_MODEL),
        'ln2_b': nrm(ks[25], (D_MODEL,), 0.02),
        'ffn2_w_gate': nrm(ks[26], (D_MODEL, D_FF), D_MODEL ** -0.5),
        'ffn2_w_up': nrm(ks[27], (D_MODEL, D_FF), D_MODEL ** -0.5),
        'ffn2_w_down': nrm(ks[28], (D_FF, D_MODEL), DEEPNORM_BETA * D_FF ** -0.5),
        'ln3_g': gain(ks[29], D_MODEL),
        'ln3_b': nrm(ks[30], (D_MODEL,), 0.02),
    }


def reference(x, c, ada_w, ada_b, ffn1_w_gate, ffn1_w_up, ffn1_w_down, ln1_g, ln1_b,
              w_in, rnn_conv_w, rnn_conv_b, rglru_w_r, rglru_b_r, rglru_w_i, rglru_b_i,
              rglru_lambda, rnn_w_proj, conf_dw_w, conf_dw_b, conf_ln_g, conf_ln_b,
              conf_w_proj, mix_w_out, ln2_g, ln2_b, ffn2_w_gate, ffn2_w_up, ffn2_w_down,
              ln3_g, ln3_b):
    mod = jax.nn.silu(c) @ ada_w + ada_b
    sh1, sc1, g1, sh2, sc2, g2, sh3, sc3, g3 = jnp.split(mod, 3 * N_SUBLAYER, axis=-1)

    for _ in range(DEPTH):
        u = modulate(x, sh1, sc1)
        y = swiglu(u, ffn1_w_gate, ffn1_w_up, ffn1_w_down)
        x = layer_norm(DEEPNORM_ALPHA * x + FFN_RES_WEIGHT * (1 + g1[:, None, :]) * y, ln1_g, ln1_b)

        u = modulate(x, sh2, sc2)
        proj = u @ w_in
        xr, gr, conf_in, gate_a, gate_b = jnp.split(proj, IN_SPLITS, axis=-1)

        xr = causal_depthwise_conv(xr, rnn_conv_w, rnn_conv_b)
        hr = rg_lru(xr, rglru_w_r, rglru_b_r, rglru_w_i, rglru_b_i, rglru_lambda)
        ya = (hr * jax.nn.gelu(gr)) @ rnn_w_proj

        glu = conf_in[..., :D_CONV] * jax.nn.sigmoid(conf_in[..., D_CONV:])
        hb = causal_depthwise_conv(glu, conf_dw_w, conf_dw_b)
        hb = jax.nn.silu(layer_norm(hb, conf_ln_g, conf_ln_b))
        yb = hb @ conf_w_proj

        merged = jax.nn.sigmoid(gate_a) * ya + jax.nn.sigmoid(gate_b) * yb
        out = merged @ mix_w_out
        x = layer_norm(DEEPNORM_ALPHA * x + (1 + g2[:, None, :]) * out, ln2_g, ln2_b)

        u = modulate(x, sh3, sc3)
        y = swiglu(u, ffn2_w_gate, ffn2_w_up, ffn2_w_down)
        x = layer_norm(DEEPNORM_ALPHA * x + FFN_RES_WEIGHT * (1 + g3[:, None, :]) * y, ln3_g, ln3_b)
    return x
```

```python
import contextlib
import numpy as np
import concourse.bass as bass
import concourse.mybir as mybir
from concourse.bass_utils import run_bass_kernel_spmd

F32 = mybir.dt.float32
BF16 = mybir.dt.bfloat16
AF = mybir.ActivationFunctionType
ALU = mybir.AluOpType

D = 1024
DFF = 2816
NF = DFF // 128
SEQ = 8192
BATCH = 4
T = 512
HALF = SEQ // 2
NCHUNK = HALF // T
ALPHA = 2.0 ** 0.25
EPS_LN = 1e-5
EPS_DN = EPS_LN / (ALPHA * ALPHA)
NR = 5
SLABW = 4096

VEC_LAYOUT = {}
_off = 0


def _reg(name, n):
    global _off
    VEC_LAYOUT[name] = (_off, n)
    _off += n


for _nm in ["ln1_g", "ln1_b", "ln2_g", "ln2_b", "ln3_g", "ln3_b", "cln_g", "cln_b",
            "c4_b", "b_r", "b_i", "lam", "c31_b", "cvec"]:
    _reg(_nm, 8)
_reg("c4_w", 32)
_reg("c31_w", 248)
_reg("ada_b", 72)
_reg("cmask", 1)
NVEC = _off


def _pk(v):
    return np.ascontiguousarray(v.reshape(-1, 128).T)


def slab_catalogue():
    slabs = []

    def add(name, **kw):
        slabs.append(dict(name=name, **kw))

    for L in (1, 2):
        for fg in range(6):
            add(f"G{L}_{fg}")
            add(f"U{L}_{fg}")
        for m in range(8):
            add(f"D{L}_{m}")
    for cs in range(12):
        add(f"WIN{cs}")
    add("GATE")
    for j in range(8):
        add(f"CONV{j}")
    for nm in ("RNNP", "CONFP", "MIXO"):
        for hf in range(2):
            add(f"{nm}{hf}")
    idx = {s["name"]: i for i, s in enumerate(slabs)}
    return slabs, idx


SLABS, SIDX = slab_catalogue()
NSLAB = len(SLABS)


class Prog:
    ENGS = ("pe", "act", "dve", "pool", "sp")

    def __init__(self, nc, stack):
        self.nc = nc
        self.stack = stack
        self.prog = {e: [] for e in self.ENGS}
        self.cnt = {e: 0 for e in self.ENGS}
        self.esem = {e: stack.enter_context(nc.semaphore(f"s_{e}")) for e in self.ENGS}
        self.known = {e: {} for e in self.ENGS}
        self.lastw = {}
        self.readers = {}
        self.dsems = {}
        self.dcnt = {}
        self.semobj = {}
        for e in self.ENGS:
            self.semobj[f"s_{e}"] = self.esem[e]

    def _deps(self, eng, reads, writes):
        deps = {}

        def add(ev):
            if ev is None:
                return
            k, v = ev
            if deps.get(k, 0) < v:
                deps[k] = v

        for s in reads:
            add(self.lastw.get(s))
        for s in writes:
            add(self.lastw.get(s))
            for ev in self.readers.get(s, {}).items():
                add(ev)
        out = []
        for k, v in deps.items():
            if eng == "pe" and k == "s_pe":
                continue
            if self.known[eng].get(k, 0) >= v:
                continue
            self.known[eng][k] = v
            out.append((k, v))
        return out

    def _record(self, ev, reads, writes):
        for s in writes:
            self.lastw[s] = ev
            self.readers[s] = {}
        for s in reads:
            r = self.readers.setdefault(s, {})
            if r.get(ev[0], 0) < ev[1]:
                r[ev[0]] = ev[1]

    def op(self, eng, fn, reads=(), writes=()):
        for w in self._deps(eng, reads, writes):
            self.prog[eng].append(("wait", w[0], w[1]))
        self.cnt[eng] += 1
        ev = (f"s_{eng}", self.cnt[eng])
        self.prog[eng].append(("op", fn, ev))
        self._record(ev, reads, writes)

    def dma(self, queue, fn, semname, reads=(), writes=()):
        if semname not in self.dsems:
            self.dsems[semname] = self.stack.enter_context(self.nc.semaphore(semname))
            self.semobj[semname] = self.dsems[semname]
            self.dcnt[semname] = 0
        for w in self._deps(queue, reads, writes):
            self.prog[queue].append(("wait", w[0], w[1]))
        self.dcnt[semname] += 16
        ev = (semname, self.dcnt[semname])
        self.prog[queue].append(("dma", fn, ev))
        self._record(ev, reads, writes)

    def cc(self, fn, reads=(), writes=()):
        semname = "s_cc"
        if semname not in self.dsems:
            self.dsems[semname] = self.stack.enter_context(self.nc.semaphore(semname))
            self.semobj[semname] = self.dsems[semname]
            self.dcnt[semname] = 0
        for w in self._deps("pool", reads, writes):
            self.prog["pool"].append(("wait", w[0], w[1]))
        self.dcnt[semname] += 1
        ev = (semname, self.dcnt[semname])
        self.prog["pool"].append(("cc", fn, ev))
        self._record(ev, reads, writes)

    def replay(self, eng, handle):
        for item in self.prog[eng]:
            if item[0] == "wait":
                handle.wait_ge(self.semobj[item[1]], item[2])
            elif item[0] == "op":
                item[1](handle).then_inc(self.semobj[item[2][0]], 1)
            elif item[0] == "cc":
                item[1](handle).then_inc(self.semobj[item[2][0]])
            else:
                item[1](handle).then_inc(self.semobj[item[2][0]], 16)

    def final_waits(self, eng, handle, evs):
        for k, v in evs:
            handle.wait_ge(self.semobj[k], v)


def build_program(nchunk_pre, nchunk_main):
    nc = bass.Bass("TRN2", target_bir_lowering=False)
    NTOK = nchunk_main * T
    NPRE = max(nchunk_pre, 1) * T

    def din(name, shape, dt=F32):
        return nc.dram_tensor(name, shape, dt, kind="ExternalInput").ap()

    x_main = din("x_main", [D, NTOK])
    x_pre = din("x_pre", [D, NPRE])
    vecs_d = din("vecs", [128, NVEC])
    ident_d = din("ident", [128, 128])
    ada_w = din("ada_w", [D, 9 * D])
    wd = {}
    for L in (1, 2):
        wd[f"g{L}"] = din(f"ffn{L}_w_gate", [D, DFF])
        wd[f"u{L}"] = din(f"ffn{L}_w_up", [D, DFF])
        wd[f"d{L}"] = din(f"ffn{L}_w_down", [DFF, D])
    wd["win"] = din("w_in", [D, 6 * D])
    wd["wr"] = din("rglru_w_r", [8, 128, 128])
    wd["wi"] = din("rglru_w_i", [8, 128, 128])
    wd["rnnp"] = din("rnn_w_proj", [D, D])
    wd["confp"] = din("conf_w_proj", [D, D])
    wd["mixo"] = din("mix_w_out", [D, D])
    y_out = nc.dram_tensor("y", [D, NTOK], F32, kind="ExternalOutput").ap()
    wscr = nc.dram_tensor("wscr", [NSLAB * 128, SLABW], BF16).ap()
    NP1 = 264
    cc_in1a = nc.dram_tensor("cc_in1a", [128, 24], F32)
    cc_out1a = nc.dram_tensor("cc_out1a", [256, 24], F32)
    cc_in1b = nc.dram_tensor("cc_in1b", [128, 240], F32)
    cc_out1b = nc.dram_tensor("cc_out1b", [256, 240], F32)
    cc_in2 = nc.dram_tensor("cc_in2", [128, 16], F32)
    cc_out2 = nc.dram_tensor("cc_out2", [256, 16], F32)
    RG = [[0, 1], [2, 3], [4, 5], [6, 7]]

    stack = contextlib.ExitStack()
    with stack:
        P = Prog(nc, stack)

        def sb(name, shape, dt):
            return stack.enter_context(nc.sbuf_tensor(name, shape, dt))

        vecs = sb("vecs_sb", [128, NVEC], F32)
        ident = sb("ident_sb", [128, 128], F32)
        ones = sb("ones_sb", [128, 128], BF16)
        modp = sb("modp", [128, 72], F32)
        cst = sb("cst", [128, 160], F32)
        carry = sb("carry", [128, 8], F32)
        xtail = sb("xtail", [128, 24], F32)
        ring = sb("ring", [128, NR * SLABW], BF16)
        A = sb("bufA", [128, 8 * T], F32)
        B = sb("bufB", [128, 8 * T], F32)
        u = sb("ubuf", [128, 8 * T], BF16)
        hbuf = sb("hbuf", [128, 24 * T], BF16)
        zb = sb("zbuf", [128, 8 * T], BF16)
        sq = sb("sqbuf", [128, 8 * T], BF16)
        st_m = sb("st_msq", [128, T], F32)
        st_r = sb("st_rstd", [128, T], F32)
        st_n = sb("st_nmr", [128, T], F32)
        XW = T + 3
        xrw = sb("xrw", [128, 8 * XW], F32)
        pay1 = sb("pay1", [128, NP1], F32)
        g1 = sb("g1", [128, 2 * NP1], F32)
        prevB = sb("prevB", [128, NP1], F32)
        tin = sb("tin", [128, NP1], F32)
        pay2 = sb("pay2", [128, 16], F32)
        g2 = sb("g2", [128, 32], F32)
        cA = sb("cA", [128, 8], F32)
        cmid = sb("cmid", [128, 8], F32)
        mine = sb("mine", [128, 8], F32)
        zeros = sb("zeros", [128, T], F32)
        gatew = sb("gatew", [128, 2 * D], BF16)
        tmp = sb("tmp", [128, 12 * T], F32)
        glu = sb("glu", [128, 8 * (T + 30)], BF16)
        sg = sb("sg", [128, 2 * T], F32)
        gat = sb("gat", [128, 6 * T], F32)
        pss = [stack.enter_context(nc.psum_tensor(f"ps{i}", [128, T], F32)) for i in range(8)]

        def vcol(name, i=0, n=1):
            o, _ = VEC_LAYOUT[name]
            return vecs[:, o + i:o + i + n]

        CST = {nm: i * 8 for i, nm in enumerate(
            ["sc1p", "cf1", "sc2p", "cf2", "sc3p", "cf3", "gs1", "bs1", "gs2", "bs2",
             "coefR", "coefR2", "silc", "t0", "t1"])}

        def ccol(name, i=0, n=1):
            return cst[:, CST[name] + i:CST[name] + i + n]

        def mcol(idx, i=0, n=1):
            return modp[:, idx * 8 + i:idx * 8 + i + n]

        psn = [0]

        def newps():
            b = psn[0] % 8
            psn[0] += 1
            return b

        def mm(out_b, lhsT, rhs, start, stop, reads, ncol=T):
            P.op("pe", lambda e, o=pss[out_b][:, 0:ncol], l=lhsT, r=rhs, s=start, t=stop:
                 e.matmul(o, l, r, start=s, stop=t), reads=reads, writes=[("ps", out_b)])

        def act(out, in_, func, reads, writes, bias=None, scale=None):
            kw = {}
            if bias is not None:
                kw["bias"] = bias
            if scale is not None:
                kw["scale"] = scale
            P.op("act", lambda e, o=out, i=in_, f=func, k=kw: e.activation(out=o, in_=i, func=f, **k),
                 reads=reads, writes=writes)

        def tt(eng, out, in0, in1, op, reads, writes):
            P.op(eng, lambda e, o=out, a=in0, b=in1, p=op: e.tensor_tensor(out=o, in0=a, in1=b, op=p),
                 reads=reads, writes=writes)

        def stt(out, in0, scalar, in1, op0, op1, reads, writes):
            P.op("dve", lambda e, o=out, a=in0, s=scalar, b=in1, p0=op0, p1=op1:
                 e.scalar_tensor_tensor(out=o, in0=a, scalar=s, in1=b, op0=p0, op1=p1),
                 reads=reads, writes=writes)

        def ts(eng, out, in0, s1, s2, op0, op1, reads, writes):
            if s2 is None:
                P.op(eng, lambda e, o=out, a=in0, x=s1, p0=op0:
                     e.tensor_scalar(out=o, in0=a, scalar1=x, scalar2=None, op0=p0),
                     reads=reads, writes=writes)
            else:
                P.op(eng, lambda e, o=out, a=in0, x=s1, y=s2, p0=op0, p1=op1:
                     e.tensor_scalar(out=o, in0=a, scalar1=x, scalar2=y, op0=p0, op1=p1),
                     reads=reads, writes=writes)

        fence_t = sb("fence_t", [128, 2], F32)

        def fence(slots):
            P.op("pool", lambda e: e.memset(fence_t[:], 0.0), reads=[], writes=list(slots) + ["fence_t"])

        def cp(eng, out, in_, reads, writes):
            P.op(eng, lambda e, o=out, i=in_: e.tensor_copy(out=o, in_=i), reads=reads, writes=writes)

        P.dma("sp", lambda e: e.dma_start(out=vecs[:], in_=vecs_d), "d_vecs", writes=["vecs"])
        P.dma("sp", lambda e: e.dma_start(out=ident[:], in_=ident_d), "d_ident", writes=["ident"])
        P.op("pool", lambda e: e.memset(ones[:], 1.0 / D), writes=["ones"])
        P.op("pool", lambda e: e.memset(zeros[:], 0.0), writes=["zeros"])
        P.op("pool", lambda e: e.memset(prevB[:], 0.0), writes=["prevBa", "prevBb"])
        P.op("pool", lambda e: e.memset(cA[:], 0.0), writes=["cA"])
        P.op("pool", lambda e: e.memset(carry[:], 0.0), writes=[("carry", j) for j in range(8)])
        P.op("pool", lambda e: e.memset(xtail[:], 0.0), writes=[("xtail", j) for j in range(8)])
        for j in range(8):
            P.op("pool", lambda e, j=j: e.memset(glu[:, j * (T + 30):j * (T + 30) + 30], 0.0),
                 writes=[("glu", j)])

        act(ccol("silc", 0, 8), vcol("cvec", 0, 8), AF.Silu, ["vecs"], ["silc"])

        def astg(k):
            return tmp[:, k * D:(k + 1) * D] if k < 6 else gat[:, (k - 6) * D:(k - 5) * D]

        def ada_range(rg):
            for k in range(8):
                P.dma("sp", lambda e, k=k, rg=rg: e.dma_start(
                    out=astg(k), in_=ada_w[k * 128:(k + 1) * 128, rg * D:(rg + 1) * D]),
                    f"d_as{k}", writes=[("astg", k)])
            pb = newps()
            for m in range(8):
                for k in range(8):
                    P.op("pe", lambda e, pb=pb, m=m, k=k: e.matmul(
                        pss[pb][:, m:m + 1], astg(k)[:, m * 128:(m + 1) * 128],
                        cst[:, CST["silc"] + k:CST["silc"] + k + 1], start=(k == 0), stop=(k == 7)),
                        reads=[("astg", k), "silc"], writes=[("ps", pb)])
            tt("dve", modp[:, rg * 8:(rg + 1) * 8], pss[pb][:, 0:8], vcol("ada_b", rg * 8, 8), ALU.add,
               [("ps", pb), "vecs"], ["modp"])

        ada_pending = list(range(9))

        def ada_next(n=1):
            for _ in range(n):
                if ada_pending:
                    ada_range(ada_pending.pop(0))

        ada_next(3)

        ostg = hbuf
        OSTG_SLOTS = [("ostg", i) for i in range(2)] + [("ostgB", i) for i in range(4)]
        cast_rr = [0]

        def cast(out, in_, reads, writes):
            eng = ("dve", "act")[cast_rr[0] % 2]
            cast_rr[0] += 1
            if eng == "act":
                act(out, in_, AF.Copy, reads, writes)
            else:
                cp(eng, out, in_, reads, writes)

        stg_n = [0]

        def stage_in(src_ap, ncols):
            q = stg_n[0] % 8
            stg_n[0] += 1
            buf = A if q < 4 else B
            dst = buf[:, (q % 4) * D:(q % 4) * D + ncols]
            P.dma("sp", lambda e, d=dst, s=src_ap: e.dma_start(out=d, in_=s), f"d_ada{q}", writes=[("stg", q)])
            return dst, ("stg", q)

        def store_slab(name, width):
            si = SIDX[name]
            return si

        def conv_typeA(W, ncols_total, names):
            ngrp = len(names)
            fence(OSTG_SLOTS)
            for g0 in range(0, ngrp, 2):
                gs = names[g0:g0 + 2]
                c0 = g0 * 512
                cw = min(1024, ncols_total - c0)
                for k in range(8):
                    src, slot = stage_in(W[k * 128:(k + 1) * 128, c0:c0 + cw], cw)
                    for gi in range(len(gs)):
                        w = min(512, cw - gi * 512)
                        if w <= 0:
                            continue
                        cast(ostg[:, gi * SLABW + k * 512:gi * SLABW + k * 512 + w],
                             src[:, gi * 512:gi * 512 + w], [slot], [("ostg", gi)])
                for gi, nm in enumerate(gs):
                    si = SIDX[nm]
                    P.dma("pool", lambda e, si=si, gi=gi: e.dma_start(
                        out=wscr[si * 128:(si + 1) * 128, :], in_=ostg[:, gi * SLABW:(gi + 1) * SLABW]),
                        f"d_wst{gi}", reads=[("ostg", gi)], writes=[("wscr", si)])

        def conv_typeB(W, L):
            fence(OSTG_SLOTS)
            for h in range(2):
                for k in range(NF):
                    src, slot = stage_in(W[k * 128:(k + 1) * 128, h * 512:(h + 1) * 512], 512)
                    for ml in range(4):
                        cast(ostg[:, ml * NF * 128 + k * 128:ml * NF * 128 + (k + 1) * 128],
                             src[:, ml * 128:(ml + 1) * 128], [slot], [("ostgB", ml)])
                for ml in range(4):
                    si = SIDX[f"D{L}_{h * 4 + ml}"]
                    P.dma("pool", lambda e, si=si, ml=ml: e.dma_start(
                        out=wscr[si * 128:(si + 1) * 128, 0:NF * 128],
                        in_=ostg[:, ml * NF * 128:(ml + 1) * NF * 128]),
                        f"d_wst{ml}", reads=[("ostgB", ml)], writes=[("wscr", si)])

        NCV = 24
        cv_n = [0]

        def cv_sem():
            i = cv_n[0] % NCV
            cv_n[0] += 1
            return f"d_cv{i}", ("cvslot", i)

        def cvtA(W, c0, w, name):
            si = SIDX[name]
            src = W[:, c0:c0 + w].rearrange("(k p) c -> p k c", p=128)
            dst = wscr[si * 128:(si + 1) * 128, 0:8 * w].rearrange("p (k c) -> p k c", k=8)
            sem, slot = cv_sem()
            P.dma("pool", lambda e, d=dst, s_=src: e.dma_start(out=d, in_=s_), sem, writes=[("wscr", si), slot])

        def cvtB(W, m, name):
            si = SIDX[name]
            src = W[:, m * 128:(m + 1) * 128].rearrange("(k p) c -> p k c", p=128)
            dst = wscr[si * 128:(si + 1) * 128, 0:NF * 128].rearrange("p (k c) -> p k c", k=NF)
            sem, slot = cv_sem()
            P.dma("pool", lambda e, d=dst, s_=src: e.dma_start(out=d, in_=s_), sem, writes=[("wscr", si), slot])

        cvt_list = []

        def cvt_ffn(L):
            for fg in range(6):
                w = 512 if fg < 5 else 256
                cvt_list.append((f"G{L}_{fg}", lambda L=L, fg=fg, w=w: cvtA(wd[f"g{L}"], fg * 512, w, f"G{L}_{fg}")))
                cvt_list.append((f"U{L}_{fg}", lambda L=L, fg=fg, w=w: cvtA(wd[f"u{L}"], fg * 512, w, f"U{L}_{fg}")))
            for m in range(8):
                cvt_list.append((f"D{L}_{m}", lambda L=L, m=m: cvtB(wd[f"d{L}"], m, f"D{L}_{m}")))

        def cvt_win(cs):
            cvt_list.append((f"WIN{cs}", lambda cs=cs: cvtA(wd["win"], cs * 512, 512, f"WIN{cs}")))

        cvt_ffn(1)
        for cs in (0, 1, 4, 6, 5, 7, 2, 3):
            cvt_win(cs)
        for hf in range(2):
            cvt_list.append((f"RNNP{hf}", lambda hf=hf: cvtA(wd["rnnp"], hf * 512, 512, f"RNNP{hf}")))
            cvt_list.append((f"CONFP{hf}", lambda hf=hf: cvtA(wd["confp"], hf * 512, 512, f"CONFP{hf}")))
            cvt_win(8 + hf)
            cvt_win(10 + hf)
        for hf in range(2):
            cvt_list.append((f"MIXO{hf}", lambda hf=hf: cvtA(wd["mixo"], hf * 512, 512, f"MIXO{hf}")))
        cvt_ffn(2)
        cvt_pos = {nm: i for i, (nm, _) in enumerate(cvt_list)}
        cvt_done = [0]
        LOOKAHEAD = 12

        def cvt_upto(idx):
            while cvt_done[0] <= idx and cvt_done[0] < len(cvt_list):
                cvt_list[cvt_done[0]][1]()
                cvt_done[0] += 1

        cvt_upto(cvt_pos["WIN1"])
        si = SIDX["GATE"]
        for gi, nm in enumerate(("wr", "wi")):
            sem, slot = cv_sem()
            P.dma("pool", lambda e, gi=gi, nm=nm, si=si: e.dma_start(
                out=wscr[si * 128:(si + 1) * 128, gi * D:(gi + 1) * D].rearrange("p (h e) -> p h e", h=8),
                in_=wd[nm].rearrange("h d e -> d h e")), sem, writes=[("wscr", si), slot])
        fence(OSTG_SLOTS)
        for j in range(8):
            gi = j % 2
            for k in range(31):
                eng = "dve"
                ts(eng, ostg[:, gi * SLABW + k * 128:gi * SLABW + (k + 1) * 128], ident[:],
                   vcol("c31_w", j * 31 + k, 1), None, ALU.mult, ALU.bypass, ["ident", "vecs"], [("ostg", gi)])
            si = SIDX[f"CONV{j}"]
            P.dma("pool", lambda e, si=si, gi=gi: e.dma_start(
                out=wscr[si * 128:(si + 1) * 128, 0:31 * 128], in_=ostg[:, gi * SLABW:gi * SLABW + 31 * 128]),
                f"d_wst{gi}", reads=[("ostg", gi)], writes=[("wscr", si)])

        ada_next(9)
        def tsc(name, in0, s1, s2, op0, op1, reads):
            ts("dve", ccol(name, 0, 8), in0, s1, s2, op0, op1, reads, [name])

        tsc("sc1p", mcol(1, 0, 8), 1.0, None, ALU.add, ALU.bypass, ["modp"])
        tsc("cf1", mcol(2, 0, 8), 1.0, 0.5 / ALPHA, ALU.add, ALU.mult, ["modp"])
        tsc("sc2p", mcol(4, 0, 8), 1.0, None, ALU.add, ALU.bypass, ["modp"])
        tsc("cf2", mcol(5, 0, 8), 1.0, 1.0 / ALPHA, ALU.add, ALU.mult, ["modp"])
        tsc("sc3p", mcol(7, 0, 8), 1.0, None, ALU.add, ALU.bypass, ["modp"])
        tsc("cf3", mcol(8, 0, 8), 1.0, 0.5 / ALPHA, ALU.add, ALU.mult, ["modp"])
        tt("dve", ccol("gs1", 0, 8), vcol("ln1_g", 0, 8), ccol("sc2p", 0, 8), ALU.mult, ["vecs", "sc2p"], ["gs1"])
        tt("dve", ccol("t0", 0, 8), vcol("ln1_b", 0, 8), ccol("sc2p", 0, 8), ALU.mult, ["vecs", "sc2p"], ["t0"])
        tt("dve", ccol("bs1", 0, 8), ccol("t0", 0, 8), mcol(3, 0, 8), ALU.add, ["t0", "modp"], ["bs1"])
        tt("dve", ccol("gs2", 0, 8), vcol("ln2_g", 0, 8), ccol("sc3p", 0, 8), ALU.mult, ["vecs", "sc3p"], ["gs2"])
        tt("dve", ccol("t1", 0, 8), vcol("ln2_b", 0, 8), ccol("sc3p", 0, 8), ALU.mult, ["vecs", "sc3p"], ["t1"])
        tt("dve", ccol("bs2", 0, 8), ccol("t1", 0, 8), mcol(6, 0, 8), ALU.add, ["t1", "modp"], ["bs2"])
        act(ccol("t0", 0, 8), vcol("lam", 0, 8), AF.Exp, ["vecs", "bs1"], ["t0"], scale=-1.0)
        act(ccol("t1", 0, 8), ccol("t0", 0, 8), AF.Ln, ["t0", "bs2"], ["t1"], bias=1.0)
        tsc("coefR", ccol("t1", 0, 8), -8.0, None, ALU.mult, ALU.bypass, ["t1"])
        tsc("coefR2", ccol("t1", 0, 8), -16.0, None, ALU.mult, ALU.bypass, ["t1"])
        CONSTS = ["vecs", "modp", "sc1p", "cf1", "sc2p", "cf2", "sc3p", "cf3", "gs1", "bs1", "gs2", "bs2",
                  "coefR", "coefR2", "ones"]

        fence([("astg", k) for k in range(8)] + [("tmp", nm, p) for nm in ["r", "a", "t1", "t2", "hh", "ge"] for p in range(2)] + [("gat", i) for i in range(4)])
        fence(OSTG_SLOTS + [("h", f) for f in range(24)] + [("stg", q) for q in range(8)]
              + [("A", m) for m in range(8)] + [("B", m) for m in range(8)])

        P.dma("sp", lambda e: e.dma_start(out=gatew[:], in_=wscr[SIDX["GATE"] * 128:(SIDX["GATE"] + 1) * 128, 0:2 * D]),
              "d_gatew", reads=[("wscr", SIDX["GATE"])], writes=["gatew"])

        ring_n = [0]
        slab_slot = {}

        def slab_width(name):
            if name[0] in "GU" and name.endswith("_5"):
                return 8 * 256
            if name.startswith("D"):
                return NF * 128
            if name == "GATE":
                return 2 * D
            if name.startswith("CONV"):
                return 31 * 128
            return SLABW

        def load_slab(name):
            if name in cvt_pos:
                cvt_upto(cvt_pos[name] + LOOKAHEAD)
            n = ring_n[0]
            ring_n[0] += 1
            r = n % NR
            si = SIDX[name]
            w = slab_width(name)
            P.dma("sp", lambda e, r=r, si=si, w=w: e.dma_start(
                out=ring[:, r * SLABW:r * SLABW + w], in_=wscr[si * 128:(si + 1) * 128, 0:w]),
                f"d_ring{r}", reads=[("wscr", si)], writes=[("ring", r)])
            slab_slot[name] = r
            return r

        def wtile(r, off):
            return ring[:, r * SLABW + off:r * SLABW + off + 128]

        def Bm(m, c0=0, c1=T):
            return B[:, m * T + c0:m * T + c1]

        def Am(m):
            return A[:, m * T:(m + 1) * T]

        def um(m):
            return u[:, m * T:(m + 1) * T]

        def tmpt(name, par):
            i = ["r", "a", "t1", "t2", "ge"].index(name)
            return tmp[:, (i * 2 + par) * T:(i * 2 + par + 1) * T], ("tmp", name, par)

        def layer_norm(buf_m, bufslot, eps, post):
            for m in range(8):
                act(zb[:, m * T:(m + 1) * T], buf_m(m), AF.Copy, [(bufslot, m)], [("zb", m)])
                act(sq[:, m * T:(m + 1) * T], buf_m(m), AF.Square, [(bufslot, m)], [("sq", m)])
            pm, pe2 = newps(), newps()
            for m in range(8):
                mm(pm, ones[:], zb[:, m * T:(m + 1) * T], m == 0, m == 7, ["ones", ("zb", m)])
            for m in range(8):
                mm(pe2, ones[:], sq[:, m * T:(m + 1) * T], m == 0, m == 7, ["ones", ("sq", m)])
            act(st_m[:], pss[pm][:], AF.Square, [("ps", pm)], ["st_m"])
            tt("dve", st_r[:], pss[pe2][:], st_m[:], ALU.subtract, [("ps", pe2), "st_m"], ["st_r"])
            act(st_r[:], st_r[:], AF.Ln, ["st_r"], ["st_r"], bias=float(eps))
            act(st_r[:], st_r[:], AF.Exp, ["st_r"], ["st_r"], scale=-0.5)
            stt(st_n[:], pss[pm][:], -1.0, st_r[:], ALU.mult, ALU.mult, [("ps", pm), "st_r"], ["st_n"])
            for m in range(8):
                tt("dve", buf_m(m), buf_m(m), st_r[:], ALU.mult, [(bufslot, m), "st_r"], [(bufslot, m)])
                tt("pool", buf_m(m), buf_m(m), st_n[:], ALU.add, [(bufslot, m), "st_n"], [(bufslot, m)])
                post(m, 0)
            for m in range(8):
                post(m, 1)

        def ffn(L, xin_m, xin_slot, cf, ln_g, ln_b, post_u, mid_hook=None):
            for fg in range(6):
                nfl = 4 if fg < 5 else 2
                ks = 512 if fg < 5 else 256
                rg = load_slab(f"G{L}_{fg}")
                ru = load_slab(f"U{L}_{fg}")
                pre_banks = {}
                if fg == 0:
                    for fl in range(2):
                        pre_banks[fl] = (newps(), newps())
                    for k in range(8):
                        for fl in range(2):
                            mm(pre_banks[fl][0], wtile(rg, k * ks + fl * 128), um(k), k == 0, k == 7,
                               [("ring", rg), ("u", k)])
                            mm(pre_banks[fl][1], wtile(ru, k * ks + fl * 128), um(k), k == 0, k == 7,
                               [("ring", ru), ("u", k)])
                for fl in range(nfl):
                    f = fg * 4 + fl
                    if fl in pre_banks:
                        pg, pu = pre_banks[fl]
                    else:
                        pg, pu = newps(), newps()
                        for k in range(8):
                            mm(pg, wtile(rg, k * ks + fl * 128), um(k), k == 0, k == 7, [("ring", rg), ("u", k)])
                        for k in range(8):
                            mm(pu, wtile(ru, k * ks + fl * 128), um(k), k == 0, k == 7, [("ring", ru), ("u", k)])
                    par = f % 2
                    sgt = sg[:, par * T:(par + 1) * T]
                    act(sgt, pss[pg][:], AF.Silu, [("ps", pg)], [("sg", par)])
                    tt("dve", hbuf[:, f * T:(f + 1) * T], pss[pu][:], sgt, ALU.mult,
                       [("ps", pu), ("sg", par)], [("h", f)])
            if mid_hook is not None:
                mid_hook()
            for m in range(8):
                rd = load_slab(f"D{L}_{m}")
                py = newps()
                for f in range(NF):
                    mm(py, wtile(rd, f * 128), hbuf[:, f * T:(f + 1) * T], f == 0, f == NF - 1,
                       [("ring", rd), ("h", f)])
                stt(Bm(m), pss[py][:], ccol(cf, m, 1), xin_m(m), ALU.mult, ALU.add,
                    [("ps", py), (xin_slot, m), cf], [("B", m)])
            layer_norm(Bm, "B", EPS_DN, post_u)

        def mixer(mode):
            GW = T + 30

            def xw(j, c0, c1):
                return xrw[:, j * XW + c0:j * XW + c1]

            xrw3 = xrw[:].rearrange("p (j c) -> p j c", j=8)
            glu3 = glu[:].rearrange("p (j c) -> p j c", j=8)
            XRW_ALL = [("xrw", j) for j in range(8)]
            GLU_ALL = [("glu", j) for j in range(8)]
            A_ALL = [("A", j) for j in range(8)]
            r0 = load_slab("WIN0")
            r1 = load_slab("WIN1")
            xr_banks = [newps() for _ in range(4)]
            for k in range(8):
                for j in range(4):
                    mm(xr_banks[j], wtile(r0, k * 512 + j * 128), um(k), k == 0, k == 7, [("ring", r0), ("u", k)])
            for j in range(8):
                if j < 4:
                    pb = xr_banks[j]
                else:
                    pb = newps()
                    for k in range(8):
                        mm(pb, wtile(r1, k * 512 + (j % 4) * 128), um(k), k == 0, k == 7, [("ring", r1), ("u", k)])
                act(xw(j, 3, XW), pss[pb][:], AF.Copy, [("ps", pb)], [("xrw", j)])
            cp("pool", pay1[:, 0:24].rearrange("p (j c) -> p j c", j=8), xrw3[:, :, T:T + 3], XRW_ALL, ["pay1a"])
            P.dma("pool", lambda e: e.dma_start(out=cc_in1a.ap(), in_=pay1[:, 0:24]), "d_cc1ai",
                  reads=["pay1a"], writes=["ccin1a"])
            P.cc(lambda e: e.collective_compute("AllGather", ALU.bypass, replica_groups=RG,
                                                ins=[cc_in1a.ap().opt()], outs=[cc_out1a.ap().opt()]),
                 reads=["ccin1a"], writes=["ccout1a"])
            for j in range(8):
                par = j % 2
                if j == 0:
                    load_slab("WIN4")
                    load_slab("WIN6")
                if j == 4:
                    load_slab("WIN5")
                    load_slab("WIN7")
                ra = slab_slot["WIN4"] if j < 4 else slab_slot["WIN5"]
                rb = slab_slot["WIN6"] if j < 4 else slab_slot["WIN7"]
                pa, pb = newps(), newps()
                for k in range(8):
                    mm(pa, wtile(ra, k * 512 + (j % 4) * 128), um(k), k == 0, k == 7, [("ring", ra), ("u", k)])
                for k in range(8):
                    mm(pb, wtile(rb, k * 512 + (j % 4) * 128), um(k), k == 0, k == 7, [("ring", rb), ("u", k)])
                sgt = sg[:, par * T:(par + 1) * T]
                act(sgt, pss[pb][:], AF.Sigmoid, [("ps", pb)], [("sg", par)])
                tt("dve", glu[:, j * GW + 30:j * GW + 30 + T], pss[pa][:], sgt, ALU.mult,
                   [("ps", pa), ("sg", par)], [("glu", j)])
            P.dma("pool", lambda e: e.dma_start(out=g1[:, 0:48].rearrange("p (r c) -> p r c", r=2),
                                                in_=cc_out1a.ap().rearrange("(r p) c -> p r c", r=2)),
                  "d_cc1ao", reads=["ccout1a"], writes=["g1a"])
            tt("dve", tin[:, 0:24], g1[:, 0:24], prevB[:, 0:24], ALU.subtract, ["g1a", "prevBa"], ["tina"])
            stt(tin[:, 0:24], tin[:, 0:24], vcol("cmask", 0, 1), prevB[:, 0:24], ALU.mult, ALU.add,
                ["tina", "prevBa", "vecs"], ["tina"])
            cp("pool", prevB[:, 0:24], g1[:, 24:48], ["g1a"], ["prevBa"])
            cp("pool", xrw3[:, :, 0:3], tin[:, 0:24].rearrange("p (j c) -> p j c", j=8), ["tina"], XRW_ALL)
            cp("pool", pay1[:, 24:NP1].rearrange("p (j c) -> p j c", j=8), glu3[:, :, T:T + 30], GLU_ALL, ["pay1b"])
            P.dma("pool", lambda e: e.dma_start(out=cc_in1b.ap(), in_=pay1[:, 24:NP1]), "d_cc1bi",
                  reads=["pay1b"], writes=["ccin1b"])
            P.cc(lambda e: e.collective_compute("AllGather", ALU.bypass, replica_groups=RG,
                                                ins=[cc_in1b.ap().opt()], outs=[cc_out1b.ap().opt()]),
                 reads=["ccin1b"], writes=["ccout1b"])
            for j in range(8):
                xrc, xrc_s = Am(j), ("A", j)
                ts("dve", xrc, xw(j, 0, T), vcol("c4_w", j * 4 + 0, 1), vcol("c4_b", j, 1), ALU.mult, ALU.add,
                   [("xrw", j), "vecs"], [xrc_s])
                for k in range(1, 4):
                    stt(xrc, xw(j, k, k + T), vcol("c4_w", j * 4 + k, 1), xrc, ALU.mult, ALU.add,
                        [("xrw", j), "vecs", xrc_s], [xrc_s])
                cp("pool", zb[:, j * T:(j + 1) * T], xrc, [xrc_s], [("zb", j)])
            P.dma("pool", lambda e: e.dma_start(out=g1[:, 48:528].rearrange("p (r c) -> p r c", r=2),
                                                in_=cc_out1b.ap().rearrange("(r p) c -> p r c", r=2)),
                  "d_cc1bo", reads=["ccout1b"], writes=["g1b"])
            tt("dve", tin[:, 24:NP1], g1[:, 48:288], prevB[:, 24:NP1], ALU.subtract, ["g1b", "prevBb"], ["tinb"])
            stt(tin[:, 24:NP1], tin[:, 24:NP1], vcol("cmask", 0, 1), prevB[:, 24:NP1], ALU.mult, ALU.add,
                ["tinb", "prevBb", "vecs"], ["tinb"])
            cp("pool", prevB[:, 24:NP1], g1[:, 288:528], ["g1b"], ["prevBb"])
            cp("pool", glu3[:, :, 0:30], tin[:, 24:NP1].rearrange("p (j c) -> p j c", j=8), ["tinb"], GLU_ALL)
            for j in range(8):
                par = j % 2
                xrc, xrc_s = Am(j), ("A", j)
                pr, pi = newps(), newps()
                mm(pr, gatew[:, j * 128:(j + 1) * 128], zb[:, j * T:(j + 1) * T], True, True, ["gatew", ("zb", j)])
                mm(pi, gatew[:, (8 + j) * 128:(9 + j) * 128], zb[:, j * T:(j + 1) * T], True, True,
                   ["gatew", ("zb", j)])
                rt, rt_s = tmpt("r", par)
                at, at_s = tmpt("a", par)
                t2, t2_s = tmpt("t2", par)
                ge, ge_s = tmpt("ge", par)
                if j < 6:
                    Pj, Pj_s = gat[:, j * T:(j + 1) * T], ("gat", j)
                else:
                    Pj, Pj_s = sg[:, (j - 6) * T:(j - 5) * T], ("sg", j - 6)
                act(rt, pss[pr][:], AF.Sigmoid, [("ps", pr), "vecs"], [rt_s], bias=vcol("b_r", j, 1))
                act(t2, pss[pi][:], AF.Sigmoid, [("ps", pi), "vecs"], [t2_s], bias=vcol("b_i", j, 1))
                act(at, rt, AF.Exp, [rt_s, "coefR"], [at_s], scale=ccol("coefR", j, 1))
                act(ge, rt, AF.Exp, [rt_s, "coefR2"], [ge_s], scale=ccol("coefR2", j, 1))
                act(ge, ge, AF.Sqrt, [ge_s], [ge_s], scale=-1.0, bias=1.0)
                tt("dve", t2, t2, xrc, ALU.mult, [t2_s, xrc_s], [t2_s])
                tt("dve", t2, t2, ge, ALU.mult, [t2_s, ge_s], [t2_s])
                P.op("dve", lambda e, hh=Am(j), at=at, t2=t2: e.tensor_tensor_scan(
                    out=hh, data0=at, data1=t2, initial=0.0, op0=ALU.mult, op1=ALU.add),
                    reads=[at_s, t2_s], writes=[("A", j)])
                P.op("dve", lambda e, pj=Pj, at=at: e.tensor_tensor_scan(
                    out=pj, data0=at, data1=zeros[:], initial=1.0, op0=ALU.mult, op1=ALU.add),
                    reads=[at_s, "zeros"], writes=[Pj_s])
                cp("pool", pay2[:, j:j + 1], A[:, j * T + T - 1:j * T + T], [("A", j)], ["pay2"])
                cp("pool", pay2[:, 8 + j:9 + j], Pj[:, T - 1:T], [Pj_s], ["pay2"])
                rc = load_slab(f"CONV{j}")
                pc = newps()
                for k in range(31):
                    mm(pc, wtile(rc, k * 128), glu[:, j * GW + k:j * GW + k + T], k == 0, k == 30,
                       [("ring", rc), ("glu", j)])
                act(xw(j, 0, T), pss[pc][:], AF.Identity, [("ps", pc), "vecs"], [("xrw", j)], bias=vcol("c31_b", j, 1))
            P.dma("pool", lambda e: e.dma_start(out=cc_in2.ap(), in_=pay2[:]), "d_cc2i", reads=["pay2"], writes=["ccin2"])
            P.cc(lambda e: e.collective_compute("AllGather", ALU.bypass, replica_groups=RG,
                                                ins=[cc_in2.ap().opt()], outs=[cc_out2.ap().opt()]),
                 reads=["ccin2"], writes=["ccout2"])
            def hbm(m):
                return xw(m, 0, T)

            def post_conf(m, ph):
                if ph == 1:
                    return
                act(hbuf[:, (8 + m) * T:(9 + m) * T], hbm(m), AF.Silu, [("xrw", m), "vecs"], [("h", 8 + m)],
                    scale=vcol("cln_g", m, 1), bias=vcol("cln_b", m, 1))

            layer_norm(hbm, "xrw", EPS_LN, post_conf)
            P.dma("pool", lambda e: e.dma_start(out=g2[:].rearrange("p (r c) -> p r c", r=2),
                                                in_=cc_out2.ap().rearrange("(r p) c -> p r c", r=2)),
                  "d_cc2o", reads=["ccout2"], writes=["g2"])
            tt("dve", cmid[:], g2[:, 8:16], cA[:], ALU.mult, ["g2", "cA"], ["cmid"])
            tt("dve", cmid[:], cmid[:], g2[:, 0:8], ALU.add, ["cmid", "g2"], ["cmid"])
            tt("dve", mine[:], cmid[:], cA[:], ALU.subtract, ["cmid", "cA"], ["mine"])
            stt(mine[:], mine[:], vcol("cmask", 0, 1), cA[:], ALU.mult, ALU.add, ["mine", "cA", "vecs"], ["mine"])
            tt("dve", cA[:], g2[:, 24:32], cmid[:], ALU.mult, ["g2", "cmid", "mine"], ["cA"])
            tt("dve", cA[:], cA[:], g2[:, 16:24], ALU.add, ["cA", "g2"], ["cA"])

            r2 = load_slab("WIN2")
            r3 = load_slab("WIN3")
            for j in range(8):
                rr = r2 if j < 4 else r3
                pgr = newps()
                for k in range(8):
                    mm(pgr, wtile(rr, k * 512 + (j % 4) * 128), um(k), k == 0, k == 7, [("ring", rr), ("u", k)])
                par = j % 2
                if j < 6:
                    Pj, Pj_s = gat[:, j * T:(j + 1) * T], ("gat", j)
                else:
                    Pj, Pj_s = sg[:, (j - 6) * T:(j - 5) * T], ("sg", j - 6)
                stt(Am(j), Pj, mine[:, j:j + 1], Am(j), ALU.mult, ALU.add, [Pj_s, "mine", ("A", j)], [("A", j)])
                t1, t1_s = tmpt("t1", par)
                t2, t2_s = tmpt("t2", par)
                act(t1, pss[pgr][:], AF.Square, [("ps", pgr)], [t1_s])
                ts("dve", t1, t1, 0.044715, 1.0, ALU.mult, ALU.add, [t1_s], [t1_s])
                tt("dve", t1, pss[pgr][:], t1, ALU.mult, [("ps", pgr), t1_s], [t1_s])
                act(t2, t1, AF.Sigmoid, [t1_s], [t2_s], scale=1.5957691216057308)
                tt("dve", t2, pss[pgr][:], t2, ALU.mult, [("ps", pgr), t2_s], [t2_s])
                tt("dve", hbuf[:, j * T:(j + 1) * T], t2, Am(j), ALU.mult, [t2_s, ("A", j)], [("h", j)])
            for hf in range(2):
                rrn = load_slab(f"RNNP{hf}")
                rcf = load_slab(f"CONFP{hf}")
                rga = load_slab(f"WIN{8 + hf}")
                rgb = load_slab(f"WIN{10 + hf}")
                for ml in range(4):
                    m = hf * 4 + ml
                    pya, pyb, pga, pgb = newps(), newps(), newps(), newps()
                    for k in range(8):
                        mm(pga, wtile(rga, k * 512 + ml * 128), um(k), k == 0, k == 7, [("ring", rga), ("u", k)])
                    for k in range(8):
                        mm(pgb, wtile(rgb, k * 512 + ml * 128), um(k), k == 0, k == 7, [("ring", rgb), ("u", k)])
                    for k in range(8):
                        mm(pyb, wtile(rcf, k * 512 + ml * 128), hbuf[:, (8 + k) * T:(9 + k) * T], k == 0, k == 7,
                           [("ring", rcf), ("h", 8 + k)])
                    for k in range(8):
                        mm(pya, wtile(rrn, k * 512 + ml * 128), hbuf[:, k * T:(k + 1) * T], k == 0, k == 7,
                           [("ring", rrn), ("h", k)])
                    par = m % 2
                    ga = gat[:, (0 + par) * T:(1 + par) * T]
                    gb = gat[:, (2 + par) * T:(3 + par) * T]
                    act(ga, pss[pga][:], AF.Sigmoid, [("ps", pga)], [("gat", par)])
                    act(gb, pss[pgb][:], AF.Sigmoid, [("ps", pgb)], [("gat", 2 + par)])
                    tt("dve", gb, pss[pyb][:], gb, ALU.mult, [("ps", pyb), ("gat", 2 + par)], [("gat", 2 + par)])
                    tt("dve", ga, pss[pya][:], ga, ALU.mult, [("ps", pya), ("gat", par)], [("gat", par)])
                    tt("pool", hbuf[:, (16 + m) * T:(17 + m) * T], ga, gb, ALU.add,
                       [("gat", par), ("gat", 2 + par)], [("h", 16 + m)])
            for hf in range(2):
                ro = load_slab(f"MIXO{hf}")
                for ml in range(4):
                    m = hf * 4 + ml
                    po = newps()
                    for k in range(8):
                        mm(po, wtile(ro, k * 512 + ml * 128), hbuf[:, (16 + k) * T:(17 + k) * T], k == 0, k == 7,
                           [("ring", ro), ("h", 16 + k)])
                    stt(Bm(m), pss[po][:], ccol("cf2", m, 1), Bm(m), ALU.mult, ALU.add,
                        [("ps", po), ("B", m), "cf2"], [("B", m)])

        def load_x(src, c):
            for m in range(8):
                P.dma("sp", lambda e, m=m, src=src, c=c: e.dma_start(
                    out=A[:, m * T:(m + 1) * T], in_=src[m * 128:(m + 1) * 128, c * T:(c + 1) * T]),
                    f"d_x{m}", writes=[("A", m)])

        def make_u1():
            for m in range(8):
                act(um(m), Am(m), AF.Identity, [("A", m), "sc1p", "modp"], [("u", m)],
                    scale=ccol("sc1p", m, 1), bias=mcol(0, m, 1))

        def post_ln1(m, ph):
            if ph == 0:
                act(um(m), Bm(m), AF.Identity, [("B", m), "gs1", "bs1"], [("u", m)],
                    scale=ccol("gs1", m, 1), bias=ccol("bs1", m, 1))
            else:
                act(Bm(m), Bm(m), AF.Identity, [("B", m), "vecs"], [("B", m)],
                    scale=vcol("ln1_g", m, 1), bias=vcol("ln1_b", m, 1))

        def post_ln2(m, ph):
            if ph == 0:
                act(um(m), Bm(m), AF.Identity, [("B", m), "gs2", "bs2"], [("u", m)],
                    scale=ccol("gs2", m, 1), bias=ccol("bs2", m, 1))
            else:
                act(Bm(m), Bm(m), AF.Identity, [("B", m), "vecs"], [("B", m)],
                    scale=vcol("ln2_g", m, 1), bias=vcol("ln2_b", m, 1))

        def post_ln3(m, ph):
            if ph == 1:
                return
            act(Bm(m), Bm(m), AF.Identity, [("B", m), "vecs"], [("B", m)],
                scale=vcol("ln3_g", m, 1), bias=vcol("ln3_b", m, 1))

        steps = [("pre" if c < nchunk_pre - 1 else "prelast", c) for c in range(nchunk_pre)]
        steps += [("full", c) for c in range(nchunk_main)]
        out_evs = []
        load_x(x_pre if steps[0][0] != "full" else x_main, steps[0][1])
        u1_done = [False]

        def hook_u1():
            make_u1()
            u1_done[0] = True

        for si_, (mode, c) in enumerate(steps):
            if not u1_done[0]:
                make_u1()
            u1_done[0] = False
            nxt = steps[si_ + 1] if si_ + 1 < len(steps) else None
            ffn(1, Am, "A", "cf1", "ln1_g", "ln1_b", post_ln1)
            if mode != "full" and nxt is not None:
                pass
            mixer(mode)
            if mode != "full":
                if nxt is not None:
                    if nxt[0] == "full" and mode == "prelast":
                        ts("pool", carry[:], carry[:], vcol("cmask", 0, 1), None, ALU.mult, ALU.bypass,
                           [("carry", j) for j in range(8)] + ["vecs"], [("carry", j) for j in range(8)])
                        ts("pool", xtail[:], xtail[:], vcol("cmask", 0, 1), None, ALU.mult, ALU.bypass,
                           [("xtail", j) for j in range(8)] + ["vecs"], [("xtail", j) for j in range(8)])
                        for j in range(8):
                            GW = T + 30
                            ts("pool", glu[:, j * GW:j * GW + 30], glu[:, j * GW:j * GW + 30],
                               vcol("cmask", 0, 1), None, ALU.mult, ALU.bypass, [("glu", j), "vecs"], [("glu", j)])
                    load_x(x_pre if nxt[0] != "full" else x_main, nxt[1])
                continue
            layer_norm(Bm, "B", EPS_DN, post_ln2)
            if nxt is not None:
                load_x(x_main, nxt[1])
            ffn(2, Bm, "B", "cf3", "ln3_g", "ln3_b", post_ln3, mid_hook=(hook_u1 if nxt is not None else None))
            for m in range(8):
                P.dma("pool", lambda e, m=m, c=c: e.dma_start(
                    out=y_out[m * 128:(m + 1) * 128, c * T:(c + 1) * T], in_=B[:, m * T:(m + 1) * T]),
                    f"d_y{m}", reads=[("B", m)], writes=[("yout", m)])
        out_evs = [(f"d_y{m}", P.dcnt[f"d_y{m}"]) for m in range(8)]

        with nc.Block() as block:
            @block.tensor
            def _(e):
                P.replay("pe", e)

            @block.scalar
            def _(e):
                P.replay("act", e)

            @block.vector
            def _(e):
                P.replay("dve", e)

            @block.gpsimd
            def _(e):
                P.replay("pool", e)
                P.final_waits("pool", e, out_evs)

            @block.sync
            def _(e):
                P.replay("sp", e)
    return nc


_CACHE = {}


def kernel(**inputs):
    f32 = np.float32
    x = np.asarray(inputs["x"], f32)
    c = np.asarray(inputs["c"], f32)
    n_cores = 8
    key = "prog"
    if key not in _CACHE:
        _CACHE[key] = build_program(0, NCHUNK)
    nc = _CACHE[key]

    vec_common = np.zeros((128, NVEC), f32)

    def put(name, arr):
        o, n = VEC_LAYOUT[name]
        assert arr.shape == (128, n), (name, arr.shape)
        vec_common[:, o:o + n] = arr

    put("ln1_g", _pk(inputs["ln1_g"])); put("ln1_b", _pk(inputs["ln1_b"]))
    put("ln2_g", _pk(inputs["ln2_g"])); put("ln2_b", _pk(inputs["ln2_b"]))
    put("ln3_g", _pk(inputs["ln3_g"])); put("ln3_b", _pk(inputs["ln3_b"]))
    put("cln_g", _pk(inputs["conf_ln_g"])); put("cln_b", _pk(inputs["conf_ln_b"]))
    put("c4_b", _pk(inputs["rnn_conv_b"])); put("b_r", _pk(inputs["rglru_b_r"]))
    put("b_i", _pk(inputs["rglru_b_i"])); put("lam", _pk(inputs["rglru_lambda"]))
    put("c31_b", _pk(inputs["conf_dw_b"]))
    w4 = np.asarray(inputs["rnn_conv_w"], f32)
    put("c4_w", np.ascontiguousarray(w4.reshape(4, 8, 128).transpose(2, 1, 0).reshape(128, 32)))
    w31 = np.asarray(inputs["conf_dw_w"], f32)
    put("c31_w", np.ascontiguousarray(w31.reshape(31, 8, 128).transpose(2, 1, 0).reshape(128, 248)))
    put("ada_b", _pk(np.asarray(inputs["ada_b"], f32)))
    ident = np.eye(128, dtype=f32)

    shared = {
        "ident": ident,
        "ada_w": np.ascontiguousarray(inputs["ada_w"], f32),
        "w_in": np.ascontiguousarray(inputs["w_in"], f32),
        "rglru_w_r": np.ascontiguousarray(inputs["rglru_w_r"], f32),
        "rglru_w_i": np.ascontiguousarray(inputs["rglru_w_i"], f32),
        "rnn_w_proj": np.ascontiguousarray(inputs["rnn_w_proj"], f32),
        "conf_w_proj": np.ascontiguousarray(inputs["conf_w_proj"], f32),
        "mix_w_out": np.ascontiguousarray(inputs["mix_w_out"], f32),
    }
    for L in (1, 2):
        for nm in ("w_gate", "w_up", "w_down"):
            shared[f"ffn{L}_{nm}"] = np.ascontiguousarray(inputs[f"ffn{L}_{nm}"], f32)

    in_maps = []
    zeros_pre = np.zeros((D, T), f32)
    for core in range(n_cores):
        b, h = core // 2, core % 2
        xt = np.ascontiguousarray(x[b].T)
        v = vec_common.copy()
        o, _ = VEC_LAYOUT["cvec"]
        v[:, o:o + 8] = _pk(c[b])
        o, _ = VEC_LAYOUT["cmask"]
        v[:, o] = float(h)
        m = dict(shared)
        m["vecs"] = v
        xc = xt.reshape(D, SEQ // T, T)[:, h::2, :]
        m["x_main"] = np.ascontiguousarray(xc.reshape(D, HALF))
        m["x_pre"] = zeros_pre
        in_maps.append(m)

    res = run_bass_kernel_spmd(nc, in_maps, core_ids=list(range(n_cores)))
    out = np.empty((BATCH, SEQ, D), f32)
    for core in range(n_cores):
        b, h = core // 2, core % 2
        yc = res.results[core]["y"].reshape(D, NCHUNK, T)
        ob = out[b].reshape(SEQ // T, T, D)
        ob[h::2] = yc.transpose(1, 2, 0)
    return out
```

```python
import contextlib
import numpy as np
import concourse.bass as bass
import concourse.mybir as mybir
from concourse.bass_utils import run_bass_kernel_spmd

F32 = mybir.dt.float32
BF16 = mybir.dt.bfloat16
AF = mybir.ActivationFunctionType
ALU = mybir.AluOpType

D = 1024
DFF = 2816
NF = DFF // 128
SEQ = 8192
BATCH = 4
T = 512
HALF = SEQ // 2
NCHUNK = HALF // T
ALPHA = 2.0 ** 0.25
EPS_LN = 1e-5
EPS_DN = EPS_LN / (ALPHA * ALPHA)
NR = 5
SLABW = 4096

VEC_LAYOUT = {}
_off = 0


def _reg(name, n):
    global _off
    VEC_LAYOUT[name] = (_off, n)
    _off += n


for _nm in ["ln1_g", "ln1_b", "ln2_g", "ln2_b", "ln3_g", "ln3_b", "cln_g", "cln_b",
            "c4_b", "b_r", "b_i", "lam", "c31_b", "cvec"]:
    _reg(_nm, 8)
_reg("c4_w", 32)
_reg("c31_w", 248)
_reg("ada_b", 72)
_reg("cmask", 1)
NVEC = _off


def _pk(v):
    return np.ascontiguousarray(v.reshape(-1, 128).T)


def slab_catalogue():
    slabs = []

    def add(name, **kw):
        slabs.append(dict(name=name, **kw))

    for L in (1, 2):
        for fg in range(6):
            add(f"G{L}_{fg}")
            add(f"U{L}_{fg}")
        for m in range(8):
            add(f"D{L}_{m}")
    for cs in range(12):
        add(f"WIN{cs}")
    add("GATE")
    for j in range(8):
        add(f"CONV{j}")
    for nm in ("RNNP", "CONFP", "MIXO"):
        for hf in range(2):
            add(f"{nm}{hf}")
    idx = {s["name"]: i for i, s in enumerate(slabs)}
    return slabs, idx


SLABS, SIDX = slab_catalogue()
NSLAB = len(SLABS)


class Prog:
    ENGS = ("pe", "act", "dve", "pool", "sp")

    def __init__(self, nc, stack):
        self.nc = nc
        self.stack = stack
        self.prog = {e: [] for e in self.ENGS}
        self.cnt = {e: 0 for e in self.ENGS}
        self.esem = {e: stack.enter_context(nc.semaphore(f"s_{e}")) for e in self.ENGS}
        self.known = {e: {} for e in self.ENGS}
        self.lastw = {}
        self.readers = {}
        self.dsems = {}
        self.dcnt = {}
        self.semobj = {}
        for e in self.ENGS:
            self.semobj[f"s_{e}"] = self.esem[e]

    def _deps(self, eng, reads, writes):
        deps = {}

        def add(ev):
            if ev is None:
                return
            k, v = ev
            if deps.get(k, 0) < v:
                deps[k] = v

        for s in reads:
            add(self.lastw.get(s))
        for s in writes:
            add(self.lastw.get(s))
            for ev in self.readers.get(s, {}).items():
                add(ev)
        out = []
        for k, v in deps.items():
            if eng == "pe" and k == "s_pe":
                continue
            if self.known[eng].get(k, 0) >= v:
                continue
            self.known[eng][k] = v
            out.append((k, v))
        return out

    def _record(self, ev, reads, writes):
        for s in writes:
            self.lastw[s] = ev
            self.readers[s] = {}
        for s in reads:
            r = self.readers.setdefault(s, {})
            if r.get(ev[0], 0) < ev[1]:
                r[ev[0]] = ev[1]

    def op(self, eng, fn, reads=(), writes=()):
        for w in self._deps(eng, reads, writes):
            self.prog[eng].append(("wait", w[0], w[1]))
        self.cnt[eng] += 1
        ev = (f"s_{eng}", self.cnt[eng])
        self.prog[eng].append(("op", fn, ev))
        self._record(ev, reads, writes)

    def dma(self, queue, fn, semname, reads=(), writes=()):
        if semname not in self.dsems:
            self.dsems[semname] = self.stack.enter_context(self.nc.semaphore(semname))
            self.semobj[semname] = self.dsems[semname]
            self.dcnt[semname] = 0
        for w in self._deps(queue, reads, writes):
            self.prog[queue].append(("wait", w[0], w[1]))
        self.dcnt[semname] += 16
        ev = (semname, self.dcnt[semname])
        self.prog[queue].append(("dma", fn, ev))
        self._record(ev, reads, writes)

    def cc(self, fn, reads=(), writes=()):
        semname = "s_cc"
        if semname not in self.dsems:
            self.dsems[semname] = self.stack.enter_context(self.nc.semaphore(semname))
            self.semobj[semname] = self.dsems[semname]
            self.dcnt[semname] = 0
        for w in self._deps("pool", reads, writes):
            self.prog["pool"].append(("wait", w[0], w[1]))
        self.dcnt[semname] += 1
        ev = (semname, self.dcnt[semname])
        self.prog["pool"].append(("cc", fn, ev))
        self._record(ev, reads, writes)

    def replay(self, eng, handle):
        for item in self.prog[eng]:
            if item[0] == "wait":
                handle.wait_ge(self.semobj[item[1]], item[2])
            elif item[0] == "op":
                item[1](handle).then_inc(self.semobj[item[2][0]], 1)
            elif item[0] == "cc":
                item[1](handle).then_inc(self.semobj[item[2][0]])
            else:
                item[1](handle).then_inc(self.semobj[item[2][0]], 16)

    def final_waits(self, eng, handle, evs):
        for k, v in evs:
            handle.wait_ge(self.semobj[k], v)


def build_program(nchunk_pre, nchunk_main):
    nc = bass.Bass("TRN2", target_bir_lowering=False)
    NTOK = nchunk_main * T
    NPRE = max(nchunk_pre, 1) * T

    def din(name, shape, dt=F32):
        return nc.dram_tensor(name, shape, dt, kind="ExternalInput").ap()

    x_main = din("x_main", [D, NTOK])
    x_pre = din("x_pre", [D, NPRE])
    vecs_d = din("vecs", [128, NVEC])
    ident_d = din("ident", [128, 128])
    ada_w = din("ada_w", [D, 9 * D])
    wd = {}
    for L in (1, 2):
        wd[f"g{L}"] = din(f"ffn{L}_w_gate", [D, DFF])
        wd[f"u{L}"] = din(f"ffn{L}_w_up", [D, DFF])
        wd[f"d{L}"] = din(f"ffn{L}_w_down", [DFF, D])
    wd["win"] = din("w_in", [D, 6 * D])
    wd["wr"] = din("rglru_w_r", [8, 128, 128])
    wd["wi"] = din("rglru_w_i", [8, 128, 128])
    wd["rnnp"] = din("rnn_w_proj", [D, D])
    wd["confp"] = din("conf_w_proj", [D, D])
    wd["mixo"] = din("mix_w_out", [D, D])
    y_out = nc.dram_tensor("y", [D, NTOK], F32, kind="ExternalOutput").ap()
    wscr = nc.dram_tensor("wscr", [NSLAB * 128, SLABW], BF16).ap()
    NP1 = 264
    cc_in1 = nc.dram_tensor("cc_in1", [128, NP1], F32)
    cc_out1 = nc.dram_tensor("cc_out1", [256, NP1], F32)
    cc_in2 = nc.dram_tensor("cc_in2", [128, 16], F32)
    cc_out2 = nc.dram_tensor("cc_out2", [256, 16], F32)
    RG = [[0, 1], [2, 3], [4, 5], [6, 7]]

    stack = contextlib.ExitStack()
    with stack:
        P = Prog(nc, stack)

        def sb(name, shape, dt):
            return stack.enter_context(nc.sbuf_tensor(name, shape, dt))

        vecs = sb("vecs_sb", [128, NVEC], F32)
        ident = sb("ident_sb", [128, 128], F32)
        ones = sb("ones_sb", [128, 128], BF16)
        modp = sb("modp", [128, 72], F32)
        cst = sb("cst", [128, 160], F32)
        carry = sb("carry", [128, 8], F32)
        xtail = sb("xtail", [128, 24], F32)
        ring = sb("ring", [128, NR * SLABW], BF16)
        A = sb("bufA", [128, 8 * T], F32)
        B = sb("bufB", [128, 8 * T], F32)
        u = sb("ubuf", [128, 8 * T], BF16)
        hbuf = sb("hbuf", [128, 24 * T], BF16)
        zb = sb("zbuf", [128, 8 * T], BF16)
        sq = sb("sqbuf", [128, 8 * T], BF16)
        st_m = sb("st_msq", [128, T], F32)
        st_r = sb("st_rstd", [128, T], F32)
        st_n = sb("st_nmr", [128, T], F32)
        XW = T + 3
        xrw = sb("xrw", [128, 8 * XW], F32)
        pay1 = sb("pay1", [128, NP1], F32)
        g1 = sb("g1", [128, 2 * NP1], F32)
        prevB = sb("prevB", [128, NP1], F32)
        tin = sb("tin", [128, NP1], F32)
        pay2 = sb("pay2", [128, 16], F32)
        g2 = sb("g2", [128, 32], F32)
        cA = sb("cA", [128, 8], F32)
        cmid = sb("cmid", [128, 8], F32)
        mine = sb("mine", [128, 8], F32)
        zeros = sb("zeros", [128, T], F32)
        gatew = sb("gatew", [128, 2 * D], BF16)
        tmp = sb("tmp", [128, 12 * T], F32)
        glu = sb("glu", [128, 8 * (T + 30)], BF16)
        sg = sb("sg", [128, 2 * T], F32)
        gat = sb("gat", [128, 6 * T], F32)
        pss = [stack.enter_context(nc.psum_tensor(f"ps{i}", [128, T], F32)) for i in range(8)]

        def vcol(name, i=0, n=1):
            o, _ = VEC_LAYOUT[name]
            return vecs[:, o + i:o + i + n]

        CST = {nm: i * 8 for i, nm in enumerate(
            ["sc1p", "cf1", "sc2p", "cf2", "sc3p", "cf3", "gs1", "bs1", "gs2", "bs2",
             "coefR", "coefR2", "silc", "t0", "t1"])}

        def ccol(name, i=0, n=1):
            return cst[:, CST[name] + i:CST[name] + i + n]

        def mcol(idx, i=0, n=1):
            return modp[:, idx * 8 + i:idx * 8 + i + n]

        psn = [0]

        def newps():
            b = psn[0] % 8
            psn[0] += 1
            return b

        def mm(out_b, lhsT, rhs, start, stop, reads, ncol=T):
            P.op("pe", lambda e, o=pss[out_b][:, 0:ncol], l=lhsT, r=rhs, s=start, t=stop:
                 e.matmul(o, l, r, start=s, stop=t), reads=reads, writes=[("ps", out_b)])

        def act(out, in_, func, reads, writes, bias=None, scale=None):
            kw = {}
            if bias is not None:
                kw["bias"] = bias
            if scale is not None:
                kw["scale"] = scale
            P.op("act", lambda e, o=out, i=in_, f=func, k=kw: e.activation(out=o, in_=i, func=f, **k),
                 reads=reads, writes=writes)

        def tt(eng, out, in0, in1, op, reads, writes):
            P.op(eng, lambda e, o=out, a=in0, b=in1, p=op: e.tensor_tensor(out=o, in0=a, in1=b, op=p),
                 reads=reads, writes=writes)

        def stt(out, in0, scalar, in1, op0, op1, reads, writes):
            P.op("dve", lambda e, o=out, a=in0, s=scalar, b=in1, p0=op0, p1=op1:
                 e.scalar_tensor_tensor(out=o, in0=a, scalar=s, in1=b, op0=p0, op1=p1),
                 reads=reads, writes=writes)

        def ts(eng, out, in0, s1, s2, op0, op1, reads, writes):
            if s2 is None:
                P.op(eng, lambda e, o=out, a=in0, x=s1, p0=op0:
                     e.tensor_scalar(out=o, in0=a, scalar1=x, scalar2=None, op0=p0),
                     reads=reads, writes=writes)
            else:
                P.op(eng, lambda e, o=out, a=in0, x=s1, y=s2, p0=op0, p1=op1:
                     e.tensor_scalar(out=o, in0=a, scalar1=x, scalar2=y, op0=p0, op1=p1),
                     reads=reads, writes=writes)

        fence_t = sb("fence_t", [128, 2], F32)

        def fence(slots):
            P.op("pool", lambda e: e.memset(fence_t[:], 0.0), reads=[], writes=list(slots) + ["fence_t"])

        def cp(eng, out, in_, reads, writes):
            P.op(eng, lambda e, o=out, i=in_: e.tensor_copy(out=o, in_=i), reads=reads, writes=writes)

        P.dma("sp", lambda e: e.dma_start(out=vecs[:], in_=vecs_d), "d_vecs", writes=["vecs"])
        P.dma("sp", lambda e: e.dma_start(out=ident[:], in_=ident_d), "d_ident", writes=["ident"])
        P.op("pool", lambda e: e.memset(ones[:], 1.0 / D), writes=["ones"])
        P.op("pool", lambda e: e.memset(zeros[:], 0.0), writes=["zeros"])
        P.op("pool", lambda e: e.memset(prevB[:], 0.0), writes=["prevB"])
        P.op("pool", lambda e: e.memset(cA[:], 0.0), writes=["cA"])
        P.op("pool", lambda e: e.memset(carry[:], 0.0), writes=[("carry", j) for j in range(8)])
        P.op("pool", lambda e: e.memset(xtail[:], 0.0), writes=[("xtail", j) for j in range(8)])
        for j in range(8):
            P.op("pool", lambda e, j=j: e.memset(glu[:, j * (T + 30):j * (T + 30) + 30], 0.0),
                 writes=[("glu", j)])

        act(ccol("silc", 0, 8), vcol("cvec", 0, 8), AF.Silu, ["vecs"], ["silc"])

        def astg(k):
            return tmp[:, k * D:(k + 1) * D] if k < 6 else gat[:, (k - 6) * D:(k - 5) * D]

        def silrep(k):
            return (st_m if k < 4 else st_r)[:, (k % 4) * 128:(k % 4 + 1) * 128]

        for k in range(8):
            act(silrep(k), ident[:], AF.Identity, ["ident", "silc"], [("silrep", k)], scale=0.0,
                bias=cst[:, CST["silc"] + k:CST["silc"] + k + 1])

        def ada_range(rg):
            for k in range(8):
                P.dma("sp", lambda e, k=k, rg=rg: e.dma_start(
                    out=astg(k), in_=ada_w[k * 128:(k + 1) * 128, rg * D:(rg + 1) * D]),
                    f"d_as{k}", writes=[("astg", k)])
            for cb in range(2):
                pb = newps()
                for k in range(8):
                    P.op("pe", lambda e, pb=pb, cb=cb, k=k: e.matmul(
                        pss[pb][:, 0:512], silrep(k), astg(k)[:, cb * 512:(cb + 1) * 512],
                        start=(k == 0), stop=(k == 7)),
                        reads=[("astg", k), ("silrep", k)], writes=[("ps", pb)])
                for c in range(4):
                    idx = rg * 8 + cb * 4 + c
                    tt("dve", st_n[:, 0:128], pss[pb][:, c * 128:(c + 1) * 128], ident[:], ALU.mult,
                       [("ps", pb), "ident"], ["st_n"])
                    P.op("dve", lambda e, idx=idx: e.reduce_sum(out=modp[:, idx:idx + 1], in_=st_n[:, 0:128],
                                                                 axis=mybir.AxisListType.X),
                         reads=["st_n"], writes=["modp"])
            tt("dve", modp[:, rg * 8:(rg + 1) * 8], modp[:, rg * 8:(rg + 1) * 8], vcol("ada_b", rg * 8, 8), ALU.add,
               ["modp", "vecs"], ["modp"])

        ada_pending = list(range(9))

        def ada_next(n=1):
            for _ in range(n):
                if ada_pending:
                    ada_range(ada_pending.pop(0))

        ada_next(3)

        ostg = hbuf
        OSTG_SLOTS = [("ostg", i) for i in range(2)] + [("ostgB", i) for i in range(4)]
        cast_rr = [0]

        def cast(out, in_, reads, writes):
            eng = ("dve", "act")[cast_rr[0] % 2]
            cast_rr[0] += 1
            if eng == "act":
                act(out, in_, AF.Copy, reads, writes)
            else:
                cp(eng, out, in_, reads, writes)

        stg_n = [0]

        def stage_in(src_ap, ncols):
            q = stg_n[0] % 8
            stg_n[0] += 1
            buf = A if q < 4 else B
            dst = buf[:, (q % 4) * D:(q % 4) * D + ncols]
            P.dma("sp", lambda e, d=dst, s=src_ap: e.dma_start(out=d, in_=s), f"d_ada{q}", writes=[("stg", q)])
            return dst, ("stg", q)

        def store_slab(name, width):
            si = SIDX[name]
            return si

        def conv_typeA(W, ncols_total, names):
            ngrp = len(names)
            fence(OSTG_SLOTS)
            for g0 in range(0, ngrp, 2):
                gs = names[g0:g0 + 2]
                c0 = g0 * 512
                cw = min(1024, ncols_total - c0)
                for k in range(8):
                    src, slot = stage_in(W[k * 128:(k + 1) * 128, c0:c0 + cw], cw)
                    for gi in range(len(gs)):
                        w = min(512, cw - gi * 512)
                        if w <= 0:
                            continue
                        cast(ostg[:, gi * SLABW + k * 512:gi * SLABW + k * 512 + w],
                             src[:, gi * 512:gi * 512 + w], [slot], [("ostg", gi)])
                for gi, nm in enumerate(gs):
                    si = SIDX[nm]
                    P.dma("pool", lambda e, si=si, gi=gi: e.dma_start(
                        out=wscr[si * 128:(si + 1) * 128, :], in_=ostg[:, gi * SLABW:(gi + 1) * SLABW]),
                        f"d_wst{gi}", reads=[("ostg", gi)], writes=[("wscr", si)])

        def conv_typeB(W, L):
            fence(OSTG_SLOTS)
            for h in range(2):
                for k in range(NF):
                    src, slot = stage_in(W[k * 128:(k + 1) * 128, h * 512:(h + 1) * 512], 512)
                    for ml in range(4):
                        cast(ostg[:, ml * NF * 128 + k * 128:ml * NF * 128 + (k + 1) * 128],
                             src[:, ml * 128:(ml + 1) * 128], [slot], [("ostgB", ml)])
                for ml in range(4):
                    si = SIDX[f"D{L}_{h * 4 + ml}"]
                    P.dma("pool", lambda e, si=si, ml=ml: e.dma_start(
                        out=wscr[si * 128:(si + 1) * 128, 0:NF * 128],
                        in_=ostg[:, ml * NF * 128:(ml + 1) * NF * 128]),
                        f"d_wst{ml}", reads=[("ostgB", ml)], writes=[("wscr", si)])

        NCV = 24
        cv_n = [0]

        def cv_sem():
            i = cv_n[0] % NCV
            cv_n[0] += 1
            return f"d_cv{i}", ("cvslot", i)

        def cvtA(W, c0, w, name):
            si = SIDX[name]
            src = W[:, c0:c0 + w].rearrange("(k p) c -> p k c", p=128)
            dst = wscr[si * 128:(si + 1) * 128, 0:8 * w].rearrange("p (k c) -> p k c", k=8)
            sem, slot = cv_sem()
            P.dma("pool", lambda e, d=dst, s_=src: e.dma_start(out=d, in_=s_), sem, writes=[("wscr", si), slot])

        def cvtB(W, m, name):
            si = SIDX[name]
            src = W[:, m * 128:(m + 1) * 128].rearrange("(k p) c -> p k c", p=128)
            dst = wscr[si * 128:(si + 1) * 128, 0:NF * 128].rearrange("p (k c) -> p k c", k=NF)
            sem, slot = cv_sem()
            P.dma("pool", lambda e, d=dst, s_=src: e.dma_start(out=d, in_=s_), sem, writes=[("wscr", si), slot])

        cvt_args = {}
        for L in (1, 2):
            for fg in range(6):
                w = 512 if fg < 5 else 256
                cvt_args[f"G{L}_{fg}"] = ("A", wd[f"g{L}"], fg * 512, w)
                cvt_args[f"U{L}_{fg}"] = ("A", wd[f"u{L}"], fg * 512, w)
            for m in range(8):
                cvt_args[f"D{L}_{m}"] = ("B", wd[f"d{L}"], m, 0)
        for cs in range(12):
            cvt_args[f"WIN{cs}"] = ("A", wd["win"], cs * 512, 512)
        for hf in range(2):
            cvt_args[f"RNNP{hf}"] = ("A", wd["rnnp"], hf * 512, 512)
            cvt_args[f"CONFP{hf}"] = ("A", wd["confp"], hf * 512, 512)
            cvt_args[f"MIXO{hf}"] = ("A", wd["mixo"], hf * 512, 512)
        direct_done = set()
        si = SIDX["GATE"]
        for gi, nm in enumerate(("wr", "wi")):
            sem, slot = cv_sem()
            P.dma("pool", lambda e, gi=gi, nm=nm, si=si: e.dma_start(
                out=wscr[si * 128:(si + 1) * 128, gi * D:(gi + 1) * D].rearrange("p (h e) -> p h e", h=8),
                in_=wd[nm].rearrange("h d e -> d h e")), sem, writes=[("wscr", si), slot])
        fence(OSTG_SLOTS)
        for j in range(8):
            gi = j % 2
            for k in range(31):
                eng = "dve"
                ts(eng, ostg[:, gi * SLABW + k * 128:gi * SLABW + (k + 1) * 128], ident[:],
                   vcol("c31_w", j * 31 + k, 1), None, ALU.mult, ALU.bypass, ["ident", "vecs"], [("ostg", gi)])
            si = SIDX[f"CONV{j}"]
            P.dma("pool", lambda e, si=si, gi=gi: e.dma_start(
                out=wscr[si * 128:(si + 1) * 128, 0:31 * 128], in_=ostg[:, gi * SLABW:gi * SLABW + 31 * 128]),
                f"d_wst{gi}", reads=[("ostg", gi)], writes=[("wscr", si)])

        ada_next(9)
        def tsc(name, in0, s1, s2, op0, op1, reads):
            ts("dve", ccol(name, 0, 8), in0, s1, s2, op0, op1, reads, [name])

        tsc("sc1p", mcol(1, 0, 8), 1.0, None, ALU.add, ALU.bypass, ["modp"])
        tsc("cf1", mcol(2, 0, 8), 1.0, 0.5 / ALPHA, ALU.add, ALU.mult, ["modp"])
        tsc("sc2p", mcol(4, 0, 8), 1.0, None, ALU.add, ALU.bypass, ["modp"])
        tsc("cf2", mcol(5, 0, 8), 1.0, 1.0 / ALPHA, ALU.add, ALU.mult, ["modp"])
        tsc("sc3p", mcol(7, 0, 8), 1.0, None, ALU.add, ALU.bypass, ["modp"])
        tsc("cf3", mcol(8, 0, 8), 1.0, 0.5 / ALPHA, ALU.add, ALU.mult, ["modp"])
        tt("dve", ccol("gs1", 0, 8), vcol("ln1_g", 0, 8), ccol("sc2p", 0, 8), ALU.mult, ["vecs", "sc2p"], ["gs1"])
        tt("dve", ccol("t0", 0, 8), vcol("ln1_b", 0, 8), ccol("sc2p", 0, 8), ALU.mult, ["vecs", "sc2p"], ["t0"])
        tt("dve", ccol("bs1", 0, 8), ccol("t0", 0, 8), mcol(3, 0, 8), ALU.add, ["t0", "modp"], ["bs1"])
        tt("dve", ccol("gs2", 0, 8), vcol("ln2_g", 0, 8), ccol("sc3p", 0, 8), ALU.mult, ["vecs", "sc3p"], ["gs2"])
        tt("dve", ccol("t1", 0, 8), vcol("ln2_b", 0, 8), ccol("sc3p", 0, 8), ALU.mult, ["vecs", "sc3p"], ["t1"])
        tt("dve", ccol("bs2", 0, 8), ccol("t1", 0, 8), mcol(6, 0, 8), ALU.add, ["t1", "modp"], ["bs2"])
        act(ccol("t0", 0, 8), vcol("lam", 0, 8), AF.Exp, ["vecs", "bs1"], ["t0"], scale=-1.0)
        act(ccol("t1", 0, 8), ccol("t0", 0, 8), AF.Ln, ["t0", "bs2"], ["t1"], bias=1.0)
        tsc("coefR", ccol("t1", 0, 8), -8.0, None, ALU.mult, ALU.bypass, ["t1"])
        tsc("coefR2", ccol("t1", 0, 8), -16.0, None, ALU.mult, ALU.bypass, ["t1"])
        CONSTS = ["vecs", "modp", "sc1p", "cf1", "sc2p", "cf2", "sc3p", "cf3", "gs1", "bs1", "gs2", "bs2",
                  "coefR", "coefR2", "ones"]

        fence([("astg", k) for k in range(8)] + [("tmp", nm, p) for nm in ["r", "a", "t1", "t2", "hh", "ge"] for p in range(2)] + [("gat", i) for i in range(4)])
        fence(["st_m", "st_r", "st_n"] + [("silrep", k) for k in range(8)])
        fence(OSTG_SLOTS + [("h", f) for f in range(24)] + [("stg", q) for q in range(8)]
              + [("A", m) for m in range(8)] + [("B", m) for m in range(8)])

        P.dma("sp", lambda e: e.dma_start(out=gatew[:], in_=wscr[SIDX["GATE"] * 128:(SIDX["GATE"] + 1) * 128, 0:2 * D]),
              "d_gatew", reads=[("wscr", SIDX["GATE"])], writes=["gatew"])

        ring_n = [0]
        slab_slot = {}

        def slab_width(name):
            if name[0] in "GU" and name.endswith("_5"):
                return 8 * 256
            if name.startswith("D"):
                return NF * 128
            if name == "GATE":
                return 2 * D
            if name.startswith("CONV"):
                return 31 * 128
            return SLABW

        def load_slab(name):
            n = ring_n[0]
            ring_n[0] += 1
            r = n % NR
            si = SIDX[name]
            w = slab_width(name)
            if name in cvt_args and name not in direct_done:
                direct_done.add(name)
                kind, W, a0, a1 = cvt_args[name]
                if kind == "A":
                    src = W[:, a0:a0 + a1].rearrange("(k p) c -> p k c", p=128)
                    dst = ring[:, r * SLABW:r * SLABW + 8 * a1].rearrange("p (k c) -> p k c", k=8)
                else:
                    src = W[:, a0 * 128:(a0 + 1) * 128].rearrange("(k p) c -> p k c", p=128)
                    dst = ring[:, r * SLABW:r * SLABW + NF * 128].rearrange("p (k c) -> p k c", k=NF)
                P.dma("pool", lambda e, d=dst, s_=src: e.dma_start(out=d, in_=s_), f"d_ring{r}",
                      writes=[("ring", r)])
                P.dma("sp", lambda e, r=r, si=si, w=w: e.dma_start(
                    out=wscr[si * 128:(si + 1) * 128, 0:w], in_=ring[:, r * SLABW:r * SLABW + w]),
                    f"d_rst{r}", reads=[("ring", r)], writes=[("wscr", si)])
                slab_slot[name] = r
                return r
            P.dma("sp", lambda e, r=r, si=si, w=w: e.dma_start(
                out=ring[:, r * SLABW:r * SLABW + w], in_=wscr[si * 128:(si + 1) * 128, 0:w]),
                f"d_ring{r}", reads=[("wscr", si)], writes=[("ring", r)])
            slab_slot[name] = r
            return r

        def wtile(r, off):
            return ring[:, r * SLABW + off:r * SLABW + off + 128]

        def Bm(m, c0=0, c1=T):
            return B[:, m * T + c0:m * T + c1]

        def Am(m):
            return A[:, m * T:(m + 1) * T]

        def um(m):
            return u[:, m * T:(m + 1) * T]

        def tmpt(name, par):
            i = ["r", "a", "t1", "t2", "ge"].index(name)
            return tmp[:, (i * 2 + par) * T:(i * 2 + par + 1) * T], ("tmp", name, par)

        def layer_norm(buf_m, bufslot, eps, post):
            for m in range(8):
                act(zb[:, m * T:(m + 1) * T], buf_m(m), AF.Copy, [(bufslot, m)], [("zb", m)])
                act(sq[:, m * T:(m + 1) * T], buf_m(m), AF.Square, [(bufslot, m)], [("sq", m)])
            pm, pe2 = newps(), newps()
            for m in range(8):
                mm(pm, ones[:], zb[:, m * T:(m + 1) * T], m == 0, m == 7, ["ones", ("zb", m)])
            for m in range(8):
                mm(pe2, ones[:], sq[:, m * T:(m + 1) * T], m == 0, m == 7, ["ones", ("sq", m)])
            act(st_m[:], pss[pm][:], AF.Square, [("ps", pm)], ["st_m"])
            tt("dve", st_r[:], pss[pe2][:], st_m[:], ALU.subtract, [("ps", pe2), "st_m"], ["st_r"])
            act(st_r[:], st_r[:], AF.Ln, ["st_r"], ["st_r"], bias=float(eps))
            act(st_r[:], st_r[:], AF.Exp, ["st_r"], ["st_r"], scale=-0.5)
            stt(st_n[:], pss[pm][:], -1.0, st_r[:], ALU.mult, ALU.mult, [("ps", pm), "st_r"], ["st_n"])
            for m in range(8):
                tt("dve", buf_m(m), buf_m(m), st_r[:], ALU.mult, [(bufslot, m), "st_r"], [(bufslot, m)])
                tt("pool", buf_m(m), buf_m(m), st_n[:], ALU.add, [(bufslot, m), "st_n"], [(bufslot, m)])
                post(m, 0)
            for m in range(8):
                post(m, 1)

        def ffn(L, xin_m, xin_slot, cf, ln_g, ln_b, post_u, mid_hook=None):
            for fg in range(6):
                nfl = 4 if fg < 5 else 2
                ks = 512 if fg < 5 else 256
                rg = load_slab(f"G{L}_{fg}")
                ru = load_slab(f"U{L}_{fg}")
                pre_banks = {}
                if fg == 0:
                    for fl in range(2):
                        pre_banks[fl] = (newps(), newps())
                    for k in range(8):
                        for fl in range(2):
                            mm(pre_banks[fl][0], wtile(rg, k * ks + fl * 128), um(k), k == 0, k == 7,
                               [("ring", rg), ("u", k)])
                            mm(pre_banks[fl][1], wtile(ru, k * ks + fl * 128), um(k), k == 0, k == 7,
                               [("ring", ru), ("u", k)])
                for fl in range(nfl):
                    f = fg * 4 + fl
                    if fl in pre_banks:
                        pg, pu = pre_banks[fl]
                    else:
                        pg, pu = newps(), newps()
                        for k in range(8):
                            mm(pg, wtile(rg, k * ks + fl * 128), um(k), k == 0, k == 7, [("ring", rg), ("u", k)])
                        for k in range(8):
                            mm(pu, wtile(ru, k * ks + fl * 128), um(k), k == 0, k == 7, [("ring", ru), ("u", k)])
                    par = f % 2
                    sgt = sg[:, par * T:(par + 1) * T]
                    act(sgt, pss[pg][:], AF.Silu, [("ps", pg)], [("sg", par)])
                    tt("dve", hbuf[:, f * T:(f + 1) * T], pss[pu][:], sgt, ALU.mult,
                       [("ps", pu), ("sg", par)], [("h", f)])
            if mid_hook is not None:
                mid_hook()
            for m in range(8):
                rd = load_slab(f"D{L}_{m}")
                py = newps()
                for f in range(NF):
                    mm(py, wtile(rd, f * 128), hbuf[:, f * T:(f + 1) * T], f == 0, f == NF - 1,
                       [("ring", rd), ("h", f)])
                stt(Bm(m), pss[py][:], ccol(cf, m, 1), xin_m(m), ALU.mult, ALU.add,
                    [("ps", py), (xin_slot, m), cf], [("B", m)])
            layer_norm(Bm, "B", EPS_DN, post_u)

        def mixer(mode):
            GW = T + 30

            def xw(j, c0, c1):
                return xrw[:, j * XW + c0:j * XW + c1]

            xrw3 = xrw[:].rearrange("p (j c) -> p j c", j=8)
            glu3 = glu[:].rearrange("p (j c) -> p j c", j=8)
            XRW_ALL = [("xrw", j) for j in range(8)]
            GLU_ALL = [("glu", j) for j in range(8)]
            A_ALL = [("A", j) for j in range(8)]
            r0 = load_slab("WIN0")
            r1 = load_slab("WIN1")
            xr_banks = [newps() for _ in range(4)]
            for k in range(8):
                for j in range(4):
                    mm(xr_banks[j], wtile(r0, k * 512 + j * 128), um(k), k == 0, k == 7, [("ring", r0), ("u", k)])
            for j in range(8):
                if j < 4:
                    pb = xr_banks[j]
                else:
                    pb = newps()
                    for k in range(8):
                        mm(pb, wtile(r1, k * 512 + (j % 4) * 128), um(k), k == 0, k == 7, [("ring", r1), ("u", k)])
                act(xw(j, 3, XW), pss[pb][:], AF.Copy, [("ps", pb)], [("xrw", j)])
            for j in range(8):
                par = j % 2
                if j == 0:
                    load_slab("WIN4")
                    load_slab("WIN6")
                if j == 4:
                    load_slab("WIN5")
                    load_slab("WIN7")
                ra = slab_slot["WIN4"] if j < 4 else slab_slot["WIN5"]
                rb = slab_slot["WIN6"] if j < 4 else slab_slot["WIN7"]
                pa, pb = newps(), newps()
                for k in range(8):
                    mm(pa, wtile(ra, k * 512 + (j % 4) * 128), um(k), k == 0, k == 7, [("ring", ra), ("u", k)])
                for k in range(8):
                    mm(pb, wtile(rb, k * 512 + (j % 4) * 128), um(k), k == 0, k == 7, [("ring", rb), ("u", k)])
                sgt = sg[:, par * T:(par + 1) * T]
                act(sgt, pss[pb][:], AF.Sigmoid, [("ps", pb)], [("sg", par)])
                tt("dve", glu[:, j * GW + 30:j * GW + 30 + T], pss[pa][:], sgt, ALU.mult,
                   [("ps", pa), ("sg", par)], [("glu", j)])
            cp("pool", pay1[:, 0:24].rearrange("p (j c) -> p j c", j=8), xrw3[:, :, T:T + 3], XRW_ALL, ["pay1"])
            cp("pool", pay1[:, 24:NP1].rearrange("p (j c) -> p j c", j=8), glu3[:, :, T:T + 30], GLU_ALL, ["pay1"])
            P.dma("pool", lambda e: e.dma_start(out=cc_in1.ap(), in_=pay1[:]), "d_cc1i", reads=["pay1"], writes=["ccin1"])
            P.cc(lambda e: e.collective_compute("AllGather", ALU.bypass, replica_groups=RG,
                                                ins=[cc_in1.ap().opt()], outs=[cc_out1.ap().opt()]),
                 reads=["ccin1"], writes=["ccout1"])
            P.dma("pool", lambda e: e.dma_start(out=g1[:].rearrange("p (r c) -> p r c", r=2),
                                                in_=cc_out1.ap().rearrange("(r p) c -> p r c", r=2)),
                  "d_cc1o", reads=["ccout1"], writes=["g1"])
            tt("dve", tin[:], g1[:, 0:NP1], prevB[:], ALU.subtract, ["g1", "prevB"], ["tin"])
            stt(tin[:], tin[:], vcol("cmask", 0, 1), prevB[:], ALU.mult, ALU.add, ["tin", "prevB", "vecs"], ["tin"])
            cp("pool", prevB[:], g1[:, NP1:2 * NP1], ["g1"], ["prevB"])
            cp("pool", xrw3[:, :, 0:3], tin[:, 0:24].rearrange("p (j c) -> p j c", j=8), ["tin"], XRW_ALL)
            cp("pool", glu3[:, :, 0:30], tin[:, 24:NP1].rearrange("p (j c) -> p j c", j=8), ["tin"], GLU_ALL)
            for j in range(8):
                xrc, xrc_s = Am(j), ("A", j)
                ts("dve", xrc, xw(j, 0, T), vcol("c4_w", j * 4 + 0, 1), vcol("c4_b", j, 1), ALU.mult, ALU.add,
                   [("xrw", j), "vecs"], [xrc_s])
                for k in range(1, 4):
                    stt(xrc, xw(j, k, k + T), vcol("c4_w", j * 4 + k, 1), xrc, ALU.mult, ALU.add,
                        [("xrw", j), "vecs", xrc_s], [xrc_s])
                cp("pool", zb[:, j * T:(j + 1) * T], xrc, [xrc_s], [("zb", j)])
            for j in range(8):
                par = j % 2
                xrc, xrc_s = Am(j), ("A", j)
                pr, pi = newps(), newps()
                mm(pr, gatew[:, j * 128:(j + 1) * 128], zb[:, j * T:(j + 1) * T], True, True, ["gatew", ("zb", j)])
                mm(pi, gatew[:, (8 + j) * 128:(9 + j) * 128], zb[:, j * T:(j + 1) * T], True, True,
                   ["gatew", ("zb", j)])
                rt, rt_s = tmpt("r", par)
                at, at_s = tmpt("a", par)
                t2, t2_s = tmpt("t2", par)
                ge, ge_s = tmpt("ge", par)
                if j < 6:
                    Pj, Pj_s = gat[:, j * T:(j + 1) * T], ("gat", j)
                else:
                    Pj, Pj_s = sg[:, (j - 6) * T:(j - 5) * T], ("sg", j - 6)
                act(rt, pss[pr][:], AF.Sigmoid, [("ps", pr), "vecs"], [rt_s], bias=vcol("b_r", j, 1))
                act(t2, pss[pi][:], AF.Sigmoid, [("ps", pi), "vecs"], [t2_s], bias=vcol("b_i", j, 1))
                act(at, rt, AF.Exp, [rt_s, "coefR"], [at_s], scale=ccol("coefR", j, 1))
                act(ge, rt, AF.Exp, [rt_s, "coefR2"], [ge_s], scale=ccol("coefR2", j, 1))
                act(ge, ge, AF.Sqrt, [ge_s], [ge_s], scale=-1.0, bias=1.0)
                tt("dve", t2, t2, xrc, ALU.mult, [t2_s, xrc_s], [t2_s])
                tt("dve", t2, t2, ge, ALU.mult, [t2_s, ge_s], [t2_s])
                P.op("dve", lambda e, hh=Am(j), at=at, t2=t2: e.tensor_tensor_scan(
                    out=hh, data0=at, data1=t2, initial=0.0, op0=ALU.mult, op1=ALU.add),
                    reads=[at_s, t2_s], writes=[("A", j)])
                P.op("dve", lambda e, pj=Pj, at=at: e.tensor_tensor_scan(
                    out=pj, data0=at, data1=zeros[:], initial=1.0, op0=ALU.mult, op1=ALU.add),
                    reads=[at_s, "zeros"], writes=[Pj_s])
                cp("pool", pay2[:, j:j + 1], A[:, j * T + T - 1:j * T + T], [("A", j)], ["pay2"])
                cp("pool", pay2[:, 8 + j:9 + j], Pj[:, T - 1:T], [Pj_s], ["pay2"])
                rc = load_slab(f"CONV{j}")
                pc = newps()
                for k in range(31):
                    mm(pc, wtile(rc, k * 128), glu[:, j * GW + k:j * GW + k + T], k == 0, k == 30,
                       [("ring", rc), ("glu", j)])
                act(xw(j, 0, T), pss[pc][:], AF.Identity, [("ps", pc), "vecs"], [("xrw", j)], bias=vcol("c31_b", j, 1))
            P.dma("pool", lambda e: e.dma_start(out=cc_in2.ap(), in_=pay2[:]), "d_cc2i", reads=["pay2"], writes=["ccin2"])
            P.cc(lambda e: e.collective_compute("AllGather", ALU.bypass, replica_groups=RG,
                                                ins=[cc_in2.ap().opt()], outs=[cc_out2.ap().opt()]),
                 reads=["ccin2"], writes=["ccout2"])
            def hbm(m):
                return xw(m, 0, T)

            def post_conf(m, ph):
                if ph == 1:
                    return
                act(hbuf[:, (8 + m) * T:(9 + m) * T], hbm(m), AF.Silu, [("xrw", m), "vecs"], [("h", 8 + m)],
                    scale=vcol("cln_g", m, 1), bias=vcol("cln_b", m, 1))

            layer_norm(hbm, "xrw", EPS_LN, post_conf)
            P.dma("pool", lambda e: e.dma_start(out=g2[:].rearrange("p (r c) -> p r c", r=2),
                                                in_=cc_out2.ap().rearrange("(r p) c -> p r c", r=2)),
                  "d_cc2o", reads=["ccout2"], writes=["g2"])
            tt("dve", cmid[:], g2[:, 8:16], cA[:], ALU.mult, ["g2", "cA"], ["cmid"])
            tt("dve", cmid[:], cmid[:], g2[:, 0:8], ALU.add, ["cmid", "g2"], ["cmid"])
            tt("dve", mine[:], cmid[:], cA[:], ALU.subtract, ["cmid", "cA"], ["mine"])
            stt(mine[:], mine[:], vcol("cmask", 0, 1), cA[:], ALU.mult, ALU.add, ["mine", "cA", "vecs"], ["mine"])
            tt("dve", cA[:], g2[:, 24:32], cmid[:], ALU.mult, ["g2", "cmid", "mine"], ["cA"])
            tt("dve", cA[:], cA[:], g2[:, 16:24], ALU.add, ["cA", "g2"], ["cA"])

            r2 = load_slab("WIN2")
            r3 = load_slab("WIN3")
            for j in range(8):
                rr = r2 if j < 4 else r3
                pgr = newps()
                for k in range(8):
                    mm(pgr, wtile(rr, k * 512 + (j % 4) * 128), um(k), k == 0, k == 7, [("ring", rr), ("u", k)])
                par = j % 2
                if j < 6:
                    Pj, Pj_s = gat[:, j * T:(j + 1) * T], ("gat", j)
                else:
                    Pj, Pj_s = sg[:, (j - 6) * T:(j - 5) * T], ("sg", j - 6)
                stt(Am(j), Pj, mine[:, j:j + 1], Am(j), ALU.mult, ALU.add, [Pj_s, "mine", ("A", j)], [("A", j)])
                t1, t1_s = tmpt("t1", par)
                t2, t2_s = tmpt("t2", par)
                act(t1, pss[pgr][:], AF.Square, [("ps", pgr)], [t1_s])
                ts("dve", t1, t1, 0.044715, 1.0, ALU.mult, ALU.add, [t1_s], [t1_s])
                tt("dve", t1, pss[pgr][:], t1, ALU.mult, [("ps", pgr), t1_s], [t1_s])
                act(t2, t1, AF.Sigmoid, [t1_s], [t2_s], scale=1.5957691216057308)
                tt("dve", t2, pss[pgr][:], t2, ALU.mult, [("ps", pgr), t2_s], [t2_s])
                tt("dve", hbuf[:, j * T:(j + 1) * T], t2, Am(j), ALU.mult, [t2_s, ("A", j)], [("h", j)])
            for hf in range(2):
                rrn = load_slab(f"RNNP{hf}")
                rcf = load_slab(f"CONFP{hf}")
                rga = load_slab(f"WIN{8 + hf}")
                rgb = load_slab(f"WIN{10 + hf}")
                for ml in range(4):
                    m = hf * 4 + ml
                    pya, pyb, pga, pgb = newps(), newps(), newps(), newps()
                    for k in range(8):
                        mm(pga, wtile(rga, k * 512 + ml * 128), um(k), k == 0, k == 7, [("ring", rga), ("u", k)])
                    for k in range(8):
                        mm(pgb, wtile(rgb, k * 512 + ml * 128), um(k), k == 0, k == 7, [("ring", rgb), ("u", k)])
                    for k in range(8):
                        mm(pyb, wtile(rcf, k * 512 + ml * 128), hbuf[:, (8 + k) * T:(9 + k) * T], k == 0, k == 7,
                           [("ring", rcf), ("h", 8 + k)])
                    for k in range(8):
                        mm(pya, wtile(rrn, k * 512 + ml * 128), hbuf[:, k * T:(k + 1) * T], k == 0, k == 7,
                           [("ring", rrn), ("h", k)])
                    par = m % 2
                    ga = gat[:, (0 + par) * T:(1 + par) * T]
                    gb = gat[:, (2 + par) * T:(3 + par) * T]
                    act(ga, pss[pga][:], AF.Sigmoid, [("ps", pga)], [("gat", par)])
                    act(gb, pss[pgb][:], AF.Sigmoid, [("ps", pgb)], [("gat", 2 + par)])
                    tt("dve", gb, pss[pyb][:], gb, ALU.mult, [("ps", pyb), ("gat", 2 + par)], [("gat", 2 + par)])
                    tt("dve", ga, pss[pya][:], ga, ALU.mult, [("ps", pya), ("gat", par)], [("gat", par)])
                    tt("pool", hbuf[:, (16 + m) * T:(17 + m) * T], ga, gb, ALU.add,
                       [("gat", par), ("gat", 2 + par)], [("h", 16 + m)])
            for hf in range(2):
                ro = load_slab(f"MIXO{hf}")
                for ml in range(4):
                    m = hf * 4 + ml
                    po = newps()
                    for k in range(8):
                        mm(po, wtile(ro, k * 512 + ml * 128), hbuf[:, (16 + k) * T:(17 + k) * T], k == 0, k == 7,
                           [("ring", ro), ("h", 16 + k)])
                    stt(Bm(m), pss[po][:], ccol("cf2", m, 1), Bm(m), ALU.mult, ALU.add,
                        [("ps", po), ("B", m), "cf2"], [("B", m)])

        def load_x(src, c):
            for m in range(8):
                P.dma("sp", lambda e, m=m, src=src, c=c: e.dma_start(
                    out=A[:, m * T:(m + 1) * T], in_=src[m * 128:(m + 1) * 128, c * T:(c + 1) * T]),
                    f"d_x{m}", writes=[("A", m)])

        def make_u1():
            for m in range(8):
                act(um(m), Am(m), AF.Identity, [("A", m), "sc1p", "modp"], [("u", m)],
                    scale=ccol("sc1p", m, 1), bias=mcol(0, m, 1))

        def post_ln1(m, ph):
            if ph == 0:
                act(um(m), Bm(m), AF.Identity, [("B", m), "gs1", "bs1"], [("u", m)],
                    scale=ccol("gs1", m, 1), bias=ccol("bs1", m, 1))
            else:
                act(Bm(m), Bm(m), AF.Identity, [("B", m), "vecs"], [("B", m)],
                    scale=vcol("ln1_g", m, 1), bias=vcol("ln1_b", m, 1))

        def post_ln2(m, ph):
            if ph == 0:
                act(um(m), Bm(m), AF.Identity, [("B", m), "gs2", "bs2"], [("u", m)],
                    scale=ccol("gs2", m, 1), bias=ccol("bs2", m, 1))
            else:
                act(Bm(m), Bm(m), AF.Identity, [("B", m), "vecs"], [("B", m)],
                    scale=vcol("ln2_g", m, 1), bias=vcol("ln2_b", m, 1))

        def post_ln3(m, ph):
            if ph == 1:
                return
            act(Bm(m), Bm(m), AF.Identity, [("B", m), "vecs"], [("B", m)],
                scale=vcol("ln3_g", m, 1), bias=vcol("ln3_b", m, 1))

        steps = [("pre" if c < nchunk_pre - 1 else "prelast", c) for c in range(nchunk_pre)]
        steps += [("full", c) for c in range(nchunk_main)]
        out_evs = []
        load_x(x_pre if steps[0][0] != "full" else x_main, steps[0][1])
        u1_done = [False]

        def hook_u1():
            make_u1()
            u1_done[0] = True

        for si_, (mode, c) in enumerate(steps):
            if not u1_done[0]:
                make_u1()
            u1_done[0] = False
            nxt = steps[si_ + 1] if si_ + 1 < len(steps) else None
            ffn(1, Am, "A", "cf1", "ln1_g", "ln1_b", post_ln1)
            if mode != "full" and nxt is not None:
                pass
            mixer(mode)
            if mode != "full":
                if nxt is not None:
                    if nxt[0] == "full" and mode == "prelast":
                        ts("pool", carry[:], carry[:], vcol("cmask", 0, 1), None, ALU.mult, ALU.bypass,
                           [("carry", j) for j in range(8)] + ["vecs"], [("carry", j) for j in range(8)])
                        ts("pool", xtail[:], xtail[:], vcol("cmask", 0, 1), None, ALU.mult, ALU.bypass,
                           [("xtail", j) for j in range(8)] + ["vecs"], [("xtail", j) for j in range(8)])
                        for j in range(8):
                            GW = T + 30
                            ts("pool", glu[:, j * GW:j * GW + 30], glu[:, j * GW:j * GW + 30],
                               vcol("cmask", 0, 1), None, ALU.mult, ALU.bypass, [("glu", j), "vecs"], [("glu", j)])
                    load_x(x_pre if nxt[0] != "full" else x_main, nxt[1])
                continue
            layer_norm(Bm, "B", EPS_DN, post_ln2)
            if nxt is not None:
                load_x(x_main, nxt[1])
            ffn(2, Bm, "B", "cf3", "ln3_g", "ln3_b", post_ln3, mid_hook=(hook_u1 if nxt is not None else None))
            for m in range(8):
                P.dma("pool", lambda e, m=m, c=c: e.dma_start(
                    out=y_out[m * 128:(m + 1) * 128, c * T:(c + 1) * T], in_=B[:, m * T:(m + 1) * T]),
                    f"d_y{m}", reads=[("B", m)], writes=[("yout", m)])
        out_evs = [(f"d_y{m}", P.dcnt[f"d_y{m}"]) for m in range(8)]

        with nc.Block() as block:
            @block.tensor
            def _(e):
                P.replay("pe", e)

            @block.scalar
            def _(e):
                P.replay("act", e)

            @block.vector
            def _(e):
                P.replay("dve", e)

            @block.gpsimd
            def _(e):
                P.replay("pool", e)
                P.final_waits("pool", e, out_evs)

            @block.sync
            def _(e):
                P.replay("sp", e)
    return nc


_CACHE = {}


def kernel(**inputs):
    f32 = np.float32
    x = np.asarray(inputs["x"], f32)
    c = np.asarray(inputs["c"], f32)
    n_cores = 8
    key = "prog"
    if key not in _CACHE:
        _CACHE[key] = build_program(0, NCHUNK)
    nc = _CACHE[key]

    vec_common = np.zeros((128, NVEC), f32)

    def put(name, arr):
        o, n = VEC_LAYOUT[name]
        assert arr.shape == (128, n), (name, arr.shape)
        vec_common[:, o:o + n] = arr

    put("ln1_g", _pk(inputs["ln1_g"])); put("ln1_b", _pk(inputs["ln1_b"]))
    put("ln2_g", _pk(inputs["ln2_g"])); put("ln2_b", _pk(inputs["ln2_b"]))
    put("ln3_g", _pk(inputs["ln3_g"])); put("ln3_b", _pk(inputs["ln3_b"]))
    put("cln_g", _pk(inputs["conf_ln_g"])); put("cln_b", _pk(inputs["conf_ln_b"]))
    put("c4_b", _pk(inputs["rnn_conv_b"])); put("b_r", _pk(inputs["rglru_b_r"]))
    put("b_i", _pk(inputs["rglru_b_i"])); put("lam", _pk(inputs["rglru_lambda"]))
    put("c31_b", _pk(inputs["conf_dw_b"]))
    w4 = np.asarray(inputs["rnn_conv_w"], f32)
    put("c4_w", np.ascontiguousarray(w4.reshape(4, 8, 128).transpose(2, 1, 0).reshape(128, 32)))
    w31 = np.asarray(inputs["conf_dw_w"], f32)
    put("c31_w", np.ascontiguousarray(w31.reshape(31, 8, 128).transpose(2, 1, 0).reshape(128, 248)))
    put("ada_b", _pk(np.asarray(inputs["ada_b"], f32)))
    ident = np.eye(128, dtype=f32)

    shared = {
        "ident": ident,
        "ada_w": np.ascontiguousarray(inputs["ada_w"], f32),
        "w_in": np.ascontiguousarray(inputs["w_in"], f32),
        "rglru_w_r": np.ascontiguousarray(inputs["rglru_w_r"], f32),
        "rglru_w_i": np.ascontiguousarray(inputs["rglru_w_i"], f32),
        "rnn_w_proj": np.ascontiguousarray(inputs["rnn_w_proj"], f32),
        "conf_w_proj": np.ascontiguousarray(inputs["conf_w_proj"], f32),
        "mix_w_out": np.ascontiguousarray(inputs["mix_w_out"], f32),
    }
    for L in (1, 2):
        for nm in ("w_gate", "w_up", "w_down"):
            shared[f"ffn{L}_{nm}"] = np.ascontiguousarray(inputs[f"ffn{L}_{nm}"], f32)

    in_maps = []
    zeros_pre = np.zeros((D, T), f32)
    for core in range(n_cores):
        b, h = core // 2, core % 2
        xt = np.ascontiguousarray(x[b].T)
        v = vec_common.copy()
        o, _ = VEC_LAYOUT["cvec"]
        v[:, o:o + 8] = _pk(c[b])
        o, _ = VEC_LAYOUT["cmask"]
        v[:, o] = float(h)
        m = dict(shared)
        m["vecs"] = v
        xc = xt.reshape(D, SEQ // T, T)[:, h::2, :]
        m["x_main"] = np.ascontiguousarray(xc.reshape(D, HALF))
        m["x_pre"] = zeros_pre
        in_maps.append(m)

    res = run_bass_kernel_spmd(nc, in_maps, core_ids=list(range(n_cores)))
    out = np.empty((BATCH, SEQ, D), f32)
    for core in range(n_cores):
        b, h = core // 2, core % 2
        yc = res.results[core]["y"].reshape(D, NCHUNK, T)
        ob = out[b].reshape(SEQ // T, T, D)
        ob[h::2] = yc.transpose(1, 2, 0)
    return out
```
